# Optimizing a Trainium2 kernel written in Bass

```python
import math
import jax, jax.numpy as jnp
from jax import lax
import numpy as np

D_MODEL = 4096
BATCH = 2
SEQ = 8192
DEPTH = 1

N_ATTN_HEADS = 8
HEAD_DIM = 128
ATTN_QK_WIDTH = N_ATTN_HEADS * 2 * HEAD_DIM
ATTN_V_WIDTH = N_ATTN_HEADS * 2 * HEAD_DIM
Q_BLOCK = 128
CONV_WIDTH = 2048
CONV_GROUPS = 16
CONV_K = 3
D_FF = 11008
FFN_CONV_K = 3
LN_EPS = 1e-5
RMS_EPS = 1e-5
ALPHA = (2.0 * DEPTH) ** 0.25
BETA = (8.0 * DEPTH) ** -0.25
IN_SPLIT_SIZES = (ATTN_QK_WIDTH, ATTN_QK_WIDTH, ATTN_V_WIDTH,
                  CONV_WIDTH, CONV_WIDTH, CONV_WIDTH, D_MODEL, D_MODEL)
IN_WIDTH = sum(IN_SPLIT_SIZES)
IN_OFFSETS = tuple(int(o) for o in np.cumsum(IN_SPLIT_SIZES)[:-1])

kernel_name = "hybrid_diffattn_shortconv_convffn_deepnorm"


def lambda_init(layer_idx):
    return 0.8 - 0.6 * math.exp(-0.3 * layer_idx)


def layer_norm(x, g, b):
    xf = x.astype(jnp.float32)
    mu = jnp.mean(xf, axis=-1, keepdims=True)
    xc = xf - mu
    var = jnp.mean(xc * xc, axis=-1, keepdims=True)
    return (xc * lax.rsqrt(var + LN_EPS) * g.astype(jnp.float32) + b.astype(jnp.float32)).astype(x.dtype)


def rms_norm(x, g):
    xf = x.astype(jnp.float32)
    ms = jnp.mean(xf * xf, axis=-1, keepdims=True)
    return (xf * lax.rsqrt(ms + RMS_EPS) * g.astype(jnp.float32)).astype(x.dtype)


def causal_dwconv(u, w):
    k_width = w.shape[0]
    s = u.shape[1]
    up = jnp.pad(u, ((0, 0), (k_width - 1, 0), (0, 0)))
    y = w[0] * up[:, 0:s]
    for j in range(1, k_width):
        y = y + w[j] * up[:, j:j + s]
    return y


def diff_attention(q, k, v, lam):
    b, s, h, _, d = q.shape
    scale = 1.0 / math.sqrt(d)
    n_blk = s // Q_BLOCK
    qt = jnp.transpose(q, (0, 2, 3, 1, 4))
    kt = jnp.transpose(k, (0, 2, 3, 1, 4))
    vt = jnp.transpose(v, (0, 2, 1, 3))
    qb = qt.reshape(b, h, 2, n_blk, Q_BLOCK, d)
    qb = jnp.moveaxis(qb, 3, 0)
    key_pos = jnp.arange(s)
    neg = jnp.finfo(jnp.float32).min

    def one_block(args):
        q_blk, i = args
        sc = jnp.einsum('bhcqd,bhckd->bhcqk', q_blk, kt).astype(jnp.float32) * scale
        q_pos = i * Q_BLOCK + jnp.arange(Q_BLOCK)
        mask = key_pos[None, :] <= q_pos[:, None]
        sc = jnp.where(mask, sc, neg)
        p = jax.nn.softmax(sc, axis=-1)
        a = p[:, :, 0] - lam.astype(jnp.float32) * p[:, :, 1]
        return jnp.einsum('bhqk,bhkv->bhqv', a.astype(vt.dtype), vt)

    out = lax.map(one_block, (qb, jnp.arange(n_blk)))
    out = jnp.transpose(out, (1, 0, 3, 2, 4))
    return out.reshape(b, s, h, 2 * d)


def setup_inputs(seed: int = 0) -> dict:
    key = jax.random.key(seed)
    ks = jax.random.split(key, 24)
    f32 = jnp.float32
    s_in = D_MODEL ** -0.5

    def nrm(k, shape, scale):
        return jax.random.normal(k, shape, f32) * scale

    x = jax.random.normal(ks[0], (BATCH, SEQ, D_MODEL), f32)
    pieces = [
        nrm(ks[1], (DEPTH, D_MODEL, ATTN_QK_WIDTH), s_in),
        nrm(ks[2], (DEPTH, D_MODEL, ATTN_QK_WIDTH), s_in),
        nrm(ks[3], (DEPTH, D_MODEL, ATTN_V_WIDTH), s_in * BETA),
        nrm(ks[4], (DEPTH, D_MODEL, CONV_WIDTH), s_in * BETA),
        nrm(ks[5], (DEPTH, D_MODEL, CONV_WIDTH), s_in),
        nrm(ks[6], (DEPTH, D_MODEL, CONV_WIDTH), s_in),
        nrm(ks[7], (DEPTH, D_MODEL, D_MODEL), s_in),
        nrm(ks[8], (DEPTH, D_MODEL, D_MODEL), s_in),
    ]
    w_in = jnp.concatenate(pieces, axis=-1)
    return {
        "x": x,
        "w_in": w_in,
        "lambda_q1": nrm(ks[9], (DEPTH, HEAD_DIM), 0.1),
        "lambda_k1": nrm(ks[10], (DEPTH, HEAD_DIM), 0.1),
        "lambda_q2": nrm(ks[11], (DEPTH, HEAD_DIM), 0.1),
        "lambda_k2": nrm(ks[12], (DEPTH, HEAD_DIM), 0.1),
        "subln_g": 1.0 + nrm(ks[13], (DEPTH, 2 * HEAD_DIM), 0.02),
        "conv_mix_w": nrm(ks[14], (DEPTH, CONV_K, CONV_WIDTH), CONV_K ** -0.5),
        "w_attn_out": nrm(ks[15], (DEPTH, ATTN_V_WIDTH, D_MODEL), ATTN_V_WIDTH ** -0.5 * BETA),
        "w_conv_out": nrm(ks[16], (DEPTH, CONV_WIDTH, D_MODEL), CONV_WIDTH ** -0.5 * BETA),
        "w_o": nrm(ks[17], (DEPTH, D_MODEL, D_MODEL), s_in * BETA),
        "ln1_g": 1.0 + nrm(ks[18], (DEPTH, D_MODEL), 0.02),
        "ln1_b": nrm(ks[19], (DEPTH, D_MODEL), 0.02),
        "w_up": nrm(ks[20], (DEPTH, D_MODEL, 2 * D_FF), s_in * BETA),
        "ffn_conv_w": nrm(ks[21], (DEPTH, FFN_CONV_K, 2 * D_FF), FFN_CONV_K ** -0.5),
        "w_down": nrm(ks[22], (DEPTH, D_FF, D_MODEL), D_FF ** -0.5 * BETA),
        "ln2_g": 1.0 + nrm(ks[23], (DEPTH, D_MODEL), 0.02),
        "ln2_b": nrm(jax.random.fold_in(ks[23], 1), (DEPTH, D_MODEL), 0.02),
    }


def reference(x, w_in, lambda_q1, lambda_k1, lambda_q2, lambda_k2, subln_g,
              conv_mix_w, w_attn_out, w_conv_out, w_o, ln1_g, ln1_b,
              w_up, ffn_conv_w, w_down, ln2_g, ln2_b):
    b, s, _ = x.shape
    h = x
    for l in range(DEPTH):
        lam_init = lambda_init(l)
        proj = jnp.einsum('bsd,de->bse', h, w_in[l])
        q, k, v, u, g_b, g_c, gate_a, gate_c = jnp.split(proj, IN_OFFSETS, axis=-1)

        q = q.reshape(b, s, N_ATTN_HEADS, 2, HEAD_DIM)
        k = k.reshape(b, s, N_ATTN_HEADS, 2, HEAD_DIM)
        v = v.reshape(b, s, N_ATTN_HEADS, 2 * HEAD_DIM)
        lam = (jnp.exp(jnp.sum(lambda_q1[l].astype(jnp.float32) * lambda_k1[l].astype(jnp.float32)))
               - jnp.exp(jnp.sum(lambda_q2[l].astype(jnp.float32) * lambda_k2[l].astype(jnp.float32)))
               + lam_init)
        attn = diff_attention(q, k, v, lam)
        attn = rms_norm(attn, subln_g[l]) * (1.0 - lam_init)
        y_a = jnp.einsum('bse,ed->bsd', attn.reshape(b, s, ATTN_V_WIDTH), w_attn_out[l])

        yc = g_b * causal_dwconv(g_c * u, conv_mix_w[l])
        y_c = jnp.einsum('bse,ed->bsd', yc, w_conv_out[l])

        merged = jax.nn.sigmoid(gate_a) * y_a + jax.nn.sigmoid(gate_c) * y_c
        mix = jnp.einsum('bsd,de->bse', merged, w_o[l])
        h = layer_norm(ALPHA * h + mix, ln1_g[l], ln1_b[l])

        z = jnp.einsum('bsd,df->bsf', h, w_up[l])
        z = causal_dwconv(z, ffn_conv_w[l])
        z_gate, z_val = jnp.split(z, 2, axis=-1)
        f = jnp.einsum('bsf,fd->bsd', jax.nn.silu(z_gate) * z_val, w_down[l])
        h = layer_norm(ALPHA * h + f, ln2_g[l], ln2_b[l])
    return h
```

```python
import math
import numpy as np
import ml_dtypes
import concourse.bass as bass
import concourse.mybir as mybir
from concourse.bass_utils import run_bass_kernel_spmd

F32 = mybir.dt.float32
BF16 = mybir.dt.bfloat16
AF = mybir.ActivationFunctionType
ALU = mybir.AluOpType
AX = mybir.AxisListType

TW = 260
HALO = 4
OWN = 256
LN_EPS = 1e-5
RMS_EPS = 1e-5
ALPHA = 2.0 ** 0.25
LAM_INIT = 0.8 - 0.6 * math.exp(-0.3 * 0)

REAL_CFG = dict(DC=32, H=8, CC=16, FC=86, NT=8)


def derive(cfg):
    c = dict(cfg)
    DC, H, CC, FC, NT = cfg['DC'], cfg['H'], cfg['CC'], cfg['FC'], cfg['NT']
    c['D'] = DC * 128
    c['SEQ'] = 4 * NT * OWN
    c['QC'] = 2 * H
    c['VW'] = H * 256
    c['CW'] = CC * 128
    c['DFF'] = FC * 128
    c['oq'] = 0
    c['ok'] = H * 256
    c['ov'] = 2 * H * 256
    c['ou'] = 3 * H * 256
    c['ogb'] = c['ou'] + c['CW']
    c['ogc'] = c['ogb'] + c['CW']
    c['oga'] = c['ogc'] + c['CW']
    c['ogcv'] = c['oga'] + c['D']
    c['INW'] = c['ogcv'] + c['D']
    c['TA'] = 1024
    c['NA'] = c['SEQ'] // c['TA']
    c['SP'] = c['SEQ'] + 128
    c['c_lam'] = 0
    c['c_sg'] = 512
    c['c_cm'] = 514
    c['c_fc'] = c['c_cm'] + CC * 3
    c['c_ln'] = c['c_fc'] + 2 * FC * 3
    c['c_fl'] = c['c_ln'] + 4 * DC
    c['NP'] = c['c_fl'] + NT
    return c


class Buf:
    __slots__ = ('name', 'w', 'r')

    def __init__(self, name):
        self.name = name
        self.w = {}
        self.r = {}


ENGS = ('pe', 'act', 'dve', 'pool', 'sp')


class Prog:
    def __init__(self, nc):
        self.nc = nc
        self.ops = {e: [] for e in ENGS}
        self.waited = {e: {} for e in ENGS}
        self.pending = {e: [] for e in ENGS}
        self.dmacnt = {}
        self.final = {e: [] for e in ENGS}

    def _need(self, eng, tok, waits):
        if tok[0] == 'E':
            _, src, idx = tok
            key = ('E', src)
            if self.waited[eng].get(key, -1) >= idx:
                return
            self.waited[eng][key] = idx
            self.ops[src][idx]['ms'] = True
            waits.append(tok)
        else:
            _, sk, val = tok
            key = ('D', sk)
            if self.waited[eng].get(key, -1) >= val:
                return
            self.waited[eng][key] = val
            waits.append(tok)

    def _deps(self, eng, mykey, reads, writes):
        waits = []
        for tok in self.pending[eng]:
            self._need(eng, tok, waits)
        self.pending[eng] = []
        for b in reads:
            for k, tok in b.w.items():
                if k == mykey and eng == 'pe':
                    continue
                self._need(eng, tok, waits)
        for b in writes:
            for k, tok in b.r.items():
                if k != mykey:
                    self._need(eng, tok, waits)
            for k, tok in b.w.items():
                if k != mykey:
                    self._need(eng, tok, waits)
        return waits

    def op(self, eng, fn, reads=(), writes=()):
        waits = self._deps(eng, eng, reads, writes)
        idx = len(self.ops[eng])
        tok = ('E', eng, idx)
        self.ops[eng].append({'fn': fn, 'waits': waits, 'ms': False, 'dma': None})
        for b in reads:
            b.r[eng] = tok
        for b in writes:
            b.w = {eng: tok}
            b.r = {}
        return tok

    def dma(self, q, out, in_, sk, reads=(), writes=()):
        mykey = ('D', sk)
        waits = self._deps(q, mykey, reads, writes)
        self.dmacnt[sk] = self.dmacnt.get(sk, 0) + 16
        tok = ('D', sk, self.dmacnt[sk])
        self.ops[q].append({'fn': (lambda e, o=out, i=in_: e.dma_start(out=o, in_=i)),
                            'waits': waits, 'ms': False, 'dma': sk})
        for b in reads:
            b.r[mykey] = tok
        for b in writes:
            b.w = {mykey: tok}
            b.r = {}
        return tok

    def barrier(self):
        toks = []
        for e in ENGS:
            if self.ops[e]:
                for idx in range(len(self.ops[e]) - 1, -1, -1):
                    if self.ops[e][idx]['dma'] is None:
                        toks.append(('E', e, idx))
                        break
        for sk, v in self.dmacnt.items():
            toks.append(('D', sk, v))
        for e in ENGS:
            self.pending[e] = list(toks)

    def final_wait(self, eng, toks):
        waits = []
        for t in toks:
            self._need(eng, t, waits)
        self.final[eng].extend(waits)

    def emit(self):
        nc = self.nc
        semval = {}
        esem = {}
        for e in ENGS:
            cnt = 0
            vals = []
            for o in self.ops[e]:
                if o['ms']:
                    cnt += 1
                vals.append(cnt)
            semval[e] = vals
            esem[e] = nc.alloc_semaphore('es_' + e)
        dsem = {sk: nc.alloc_semaphore('ds_' + sk) for sk in self.dmacnt}
        self.maxvals = {e: (semval[e][-1] if semval[e] else 0) for e in ENGS}

        def run(ename, eh):
            def dowait(tok):
                if tok[0] == 'E':
                    eh.wait_ge(esem[tok[1]], semval[tok[1]][tok[2]])
                else:
                    eh.wait_ge(dsem[tok[1]], tok[2])
            for o in self.ops[ename]:
                for tok in o['waits']:
                    dowait(tok)
                ins = o['fn'](eh)
                if o['dma'] is not None:
                    ins.then_inc(dsem[o['dma']], 16)
                elif o['ms']:
                    ins.then_inc(esem[ename], 1)
            for tok in self.final[ename]:
                dowait(tok)

        with nc.Block() as block:
            @block.tensor
            def _(e):
                run('pe', e)

            @block.scalar
            def _(e):
                run('act', e)

            @block.vector
            def _(e):
                run('dve', e)

            @block.gpsimd
            def _(e):
                run('pool', e)

            @block.sync
            def _(e):
                run('sp', e)


def _build(cfg, plan, phases=('A', 'B')):
    c = derive(cfg)
    DC, H, CC, FC, NT = c['DC'], c['H'], c['CC'], c['FC'], c['NT']
    D, SEQ, QC, VW, CW, DFF = c['D'], c['SEQ'], c['QC'], c['VW'], c['CW'], c['DFF']
    TA, NA, SP, NPAR = c['TA'], c['NA'], c['SP'], c['NP']
    assert CC % 2 == 0 and DC % 2 == 0 and QC % 2 == 0

    nc = bass.Bass("TRN2", target_bir_lowering=False)
    P = Prog(nc)

    xT_full = nc.dram_tensor("xT_full", [D, SEQ], F32, kind="ExternalInput").ap()
    xT_tiles = nc.dram_tensor("xT_tiles", [NT, D, TW], F32, kind="ExternalInput").ap()
    params_d = nc.dram_tensor("params", [128, NPAR], F32, kind="ExternalInput").ap()
    masks_d = nc.dram_tensor("masks", [NT, 128, 9 * TW], BF16, kind="ExternalInput").ap()
    yT = nc.dram_tensor("yT", [NT, D, OWN], F32, kind="ExternalOutput").ap()
    KT = nc.dram_tensor("kt_scr", [QC, 128, SP], BF16).ap()
    VS = nc.dram_tensor("v_scr", [SP, VW], BF16).ap()

    xT_full_v = xT_full.rearrange("(c p) t -> p c t", p=128)

    RC = max(FC, 2 * QC + CC + DC)
    KVSZ = 2 * 9 * 128 + 9 * 256
    NPT = 6
    BIGSZ = max(DC * TA, RC * TW + 2 * KVSZ + NPT * TW)
    big = nc.alloc_sbuf_tensor("big", [128, BIGSZ], BF16)
    hb_flat = nc.alloc_sbuf_tensor("hb", [128, DC * TW], BF16)
    hb = hb_flat[:, :].rearrange("p (c t) -> p c t", t=TW)
    NST = 6
    assert DC * TW >= NST * (512 + 256)
    tt_ = nc.alloc_sbuf_tensor("tres", [128, DC, TW], F32)
    WSZ = max(DC * 256, (QC + CC) * 256, ((FC + 1) // 2) * 128)
    NS = 4
    wsl = [nc.alloc_sbuf_tensor("wsl%d" % i, [128, WSZ], BF16) for i in range(NS)]
    NSC = 10
    scrt = [nc.alloc_sbuf_tensor("scr%d" % i, [128, TW], F32) for i in range(NSC)]
    maskt = nc.alloc_sbuf_tensor("maskt", [128, 9, TW], BF16)
    o1buf = [nc.alloc_sbuf_tensor("o1buf%d" % i, [128, TW], F32) for i in range(2)]
    o1B = [Buf('o1b%d' % i) for i in range(2)]
    ln_mean = nc.alloc_sbuf_tensor("ln_mean", [128, TW], F32)
    ln_rstd = nc.alloc_sbuf_tensor("ln_rstd", [128, TW], F32)
    ln_meanB = Buf('ln_mean')
    ln_rstdB = Buf('ln_rstd')
    par = nc.alloc_sbuf_tensor("par", [128, NPAR], F32)
    lamt = nc.alloc_sbuf_tensor("lamt", [128, 8], F32)
    ones_f = nc.alloc_sbuf_tensor("ones_f", [128, 128], F32)
    ones_b = nc.alloc_sbuf_tensor("ones_b", [128, 128], BF16)
    zero_b = nc.alloc_sbuf_tensor("zero_b", [128, 256], BF16)
    eps_t = nc.alloc_sbuf_tensor("eps_t", [128, 1], F32)
    kst = [hb_flat[:, 512 * i:512 * (i + 1)] for i in range(NST)]
    vst = [hb_flat[:, 512 * NST + 256 * i:512 * NST + 256 * (i + 1)] for i in range(NST)]
    psum = [nc.alloc_psum_tensor("ps%d" % i, [128, 512], F32) for i in range(8)]

    xA = big[:, 0:DC * TA].rearrange("p (c t) -> p c t", t=TA)
    Rv = big[:, 0:RC * TW].rearrange("p (c t) -> p c t", t=TW)
    kvoff = RC * TW
    NKV = 4
    KVS = 5 * 128 + 5 * 256
    assert NKV * KVS <= 2 * KVSZ
    kt_sl = [big[:, kvoff + i * KVS: kvoff + i * KVS + 640] for i in range(NKV)]
    v_sl = [big[:, kvoff + i * KVS + 640: kvoff + (i + 1) * KVS].rearrange("p (b v) -> p b v", v=256) for i in range(NKV)]
    ptoff = kvoff + 2 * KVSZ
    pt = [big[:, ptoff + i * TW: ptoff + (i + 1) * TW] for i in range(NPT)]

    xAB = Buf('xA')
    RB = [Buf('R%d' % i) for i in range(RC)]
    hbB = [Buf('hb%d' % i) for i in range(DC)]
    tB = [Buf('t%d' % i) for i in range(DC)]
    wB = [Buf('w%d' % i) for i in range(NS)]
    scrB = [Buf('scr%d' % i) for i in range(NSC)]
    maskB = Buf('mask')
    parB = Buf('par')
    lamB = Buf('lam')
    constB = Buf('const')
    kstB = [Buf('kst%d' % i) for i in range(NST)]
    vstB = [Buf('vst%d' % i) for i in range(NST)]
    psB = [Buf('ps%d' % i) for i in range(8)]
    kvB = [Buf('kv%d' % i) for i in range(4)]
    ptB = [Buf('pt%d' % i) for i in range(NPT)]
    ktdB = Buf('ktd')

    st = {'ps': 0, 'scr': 0, 'w': 0, 'kv': 0, 'pt': 0, 'ev': 0, 'kst': 0, 'vst': 0, 'sb': 0}

    reserved = set()

    def ps():
        while True:
            i = st['ps'] % 8
            st['ps'] += 1
            if i not in reserved:
                return psum[i], psB[i]

    def scr():
        i = st['scr'] % NSC
        st['scr'] += 1
        return scrt[i], scrB[i]

    dry = plan is None
    rec = {}
    if not dry:
        keys = list(plan.keys())
        slabidx = {k: i for i, k in enumerate(keys)}
        wsrc = nc.dram_tensor("wsrc", [len(keys), 128, WSZ], F32, kind="ExternalInput").ap()
        SPT = 100
        wscr_t = [nc.dram_tensor("wscr%d" % g, [min(SPT, len(keys) - g * SPT), 128, WSZ], BF16).ap()
                  for g in range((len(keys) + SPT - 1) // SPT)]

        class _W:
            def __getitem__(self, k):
                return wscr_t[k // SPT][k % SPT]
        wscr = _W()
        batch_of = {}
        convB = {}
        conv_pending = [k for k in keys if k[0] not in ('K', 'V', 'dn', 'up')]

    def emit_conv_key(key, bname):
        if bname not in convB:
            convB[bname] = Buf(bname)
        k = slabidx[key]
        n = max(doff + nk * ncol for (doff, sname, kc0, nk, col0, ncol) in plan[key])
        P.dma('pool', wscr[k][:, 0:n], wsrc[k][:, 0:n], bname, writes=[convB[bname]])
        batch_of[key] = bname

    def pace_conv(bname, cnt=1):
        for _ in range(cnt):
            if conv_pending:
                emit_conv_key(conv_pending.pop(0), bname)

    def wload(key, parts):
        i = st['w'] % NS
        st['w'] += 1
        if dry:
            rec.setdefault(key, parts)
        else:
            n = max(doff + nk * ncol for (doff, sname, kc0, nk, col0, ncol) in parts)
            k = slabidx[key]
            if key not in batch_of:
                bname = 'cvF' + key[0]
                if bname not in convB:
                    convB[bname] = Buf(bname)
                P.dma('pool', wsl[i][:, 0:n], wsrc[k][:, 0:n], 'wc%d' % i, writes=[wB[i]])
                P.dma('sp', wscr[k][:, 0:n], wsl[i][:, 0:n], bname, reads=[wB[i]], writes=[convB[bname]])
                batch_of[key] = bname
            else:
                P.dma('sp', wsl[i][:, 0:n], wscr[k][:, 0:n], 'w%d' % i, reads=[convB[batch_of[key]]],
                      writes=[wB[i]])
        return wsl[i], wB[i]

    def v3(off, nk, ncol):
        return lambda t: t[:, off:off + nk * ncol].rearrange("p (c n) -> p c n", n=ncol)

    def mm(out, lhsT, rhs, start, stop, reads, writes):
        P.op('pe', lambda e, o=out, l=lhsT, r=rhs, s=start, p=stop: e.matmul(o, l, r, start=s, stop=p),
             reads, writes)

    def act(out, in_, func, reads, writes, bias=None, scale=None):
        kw = {}
        if bias is not None:
            kw['bias'] = bias
        if scale is not None:
            kw['scale'] = scale
        P.op('act', lambda e, o=out, i=in_, f=func, k=kw: e.activation(o, i, f, **k), reads, writes)

    def dtt(out, in0, in1, op, reads, writes):
        P.op('dve', lambda e, o=out, a=in0, b=in1, p=op: e.tensor_tensor(o, a, b, p), reads, writes)

    def dts(out, in0, s1, s2, op0, op1, reads, writes):
        if op1 is None:
            P.op('dve', lambda e, o=out, a=in0, x=s1, p0=op0: e.tensor_scalar(o, a, x, None, p0), reads, writes)
        else:
            P.op('dve', lambda e, o=out, a=in0, x=s1, y=s2, p0=op0, p1=op1: e.tensor_scalar(o, a, x, y, p0, p1),
                 reads, writes)

    def dstt(out, in0, scalar, in1, op0, op1, reads, writes):
        P.op('dve', lambda e, o=out, a=in0, s=scalar, b=in1, p0=op0, p1=op1:
             e.scalar_tensor_tensor(o, a, s, b, p0, p1), reads, writes)

    def evac(out, in_, reads, writes):
        st['ev'] += 1
        if st['ev'] % 2:
            act(out, in_, AF.Copy, reads, writes)
        else:
            P.op('dve', lambda e, o=out, i=in_: e.tensor_copy(o, i), reads, writes)

    def pcol(col):
        return par[:, col:col + 1]

    P.op('dve', lambda e: e.memset(ones_f[:, :], 1.0), (), [constB])
    P.op('dve', lambda e: e.memset(ones_b[:, :], 1.0), (), [constB])
    P.op('dve', lambda e: e.memset(zero_b[:, :], 0.0), (), [constB])
    assert LN_EPS == RMS_EPS
    P.op('dve', lambda e: e.memset(eps_t[:, :], LN_EPS), (), [constB])
    P.dma('pool', par[:, :], params_d, 'ld_par', writes=[parB])
    s0, s0B = scr()
    for k in range(2):
        dtt(s0[:, 0:128], par[:, 256 * k:256 * k + 128], par[:, 256 * k + 128:256 * k + 256], ALU.mult,
            [parB], [s0B])
        P.op('dve', lambda e, k=k: e.reduce_sum(lamt[:, k:k + 1], s0[:, 0:128], AX.X), [s0B], [lamB])
    act(lamt[:, 2:4], lamt[:, 0:2], AF.Exp, [lamB], [lamB])
    dtt(lamt[:, 4:5], lamt[:, 2:3], lamt[:, 3:4], ALU.subtract, [lamB], [lamB])
    dts(lamt[:, 5:6], lamt[:, 4:5], LAM_INIT, -1.0, ALU.add, ALU.mult, [lamB], [lamB])
    dts(lamt[:, 6:8], par[:, c['c_sg']:c['c_sg'] + 2], 1.0 - LAM_INIT, None, ALU.mult, None, [parB, lamB], [lamB])
    neglam = lamt[:, 5:6]

    if 'A' in phases:
        for ch in range(QC):
            P.dma('pool', KT[ch, :, 0:128], zero_b[:, 0:128], 'st_pad', reads=[constB])
        for s in range(VW // 256):
            P.dma('pool', VS[0:128, 256 * s:256 * s + 256], zero_b[:, :], 'st_pad', reads=[constB])
        for a in range(NA):
            g8 = max(1, DC // 4)
            for c0 in range(0, DC, g8):
                P.dma('pool', xA[:, c0:c0 + g8, :], xT_full_v[:, c0:c0 + g8, a * TA:(a + 1) * TA], 'ld_xa',
                      writes=[xAB])
            if not dry and a == 0:
                for key in keys:
                    if key[0] in ('K', 'V'):
                        emit_conv_key(key, 'cvA%s%d' % (key[0], key[1]))
                every = max(1, (NA * (QC * (TA // 512) + (VW // 256) * (TA // 128))) // max(1, len(conv_pending)))
            for s in range(QC // 2):
                col0 = c['ok'] + 256 * s
                wt, wb = wload(('K', s), [(0, 'in', 0, DC, col0, 256)])
                wv = v3(0, DC, 256)(wt)
                for j in range(2):
                    ch = 2 * s + j
                    for t5 in range(TA // 512):
                        bank, bB = ps()
                        for kc in range(DC):
                            mm(bank[:, 0:512], wv[:, kc, j * 128:(j + 1) * 128], xA[:, kc, t5 * 512:(t5 + 1) * 512],
                               kc == 0, kc == DC - 1, [wb, xAB], [bB])
                        n = st['kst'] % NST
                        st['kst'] += 1
                        evac(kst[n][:, :], bank[:, 0:512], [bB], [kstB[n]])
                        off = 128 + a * TA + t5 * 512
                        P.dma('pool', KT[ch, :, off:off + 512], kst[n][:, :], 'st_k%d' % n, reads=[kstB[n]])
                        st['nst'] = st.get('nst', 0) + 1
                        if not dry and st['nst'] % every == 0:
                            pace_conv('cv%d' % a)
            for s in range(VW // 256):
                col0 = c['ov'] + 256 * s
                wt, wb = wload(('V', s), [(0, 'in', 0, DC, col0, 256)])
                wv = v3(0, DC, 256)(wt)
                for tb in range(TA // 128):
                    bank, bB = ps()
                    for kc in range(DC):
                        mm(bank[:, 0:256], xA[:, kc, tb * 128:(tb + 1) * 128], wv[:, kc, :],
                           kc == 0, kc == DC - 1, [wb, xAB], [bB])
                    n = st['vst'] % NST
                    st['vst'] += 1
                    evac(vst[n][:, :], bank[:, 0:256], [bB], [vstB[n]])
                    off = 128 + a * TA + tb * 128
                    P.dma('pool', VS[off:off + 128, 256 * s:256 * s + 256], vst[n][:, :], 'st_v%d' % n,
                          reads=[vstB[n]])
                    st['nst'] = st.get('nst', 0) + 1
                    if not dry and st['nst'] % every == 0:
                        pace_conv('cv%d' % a)
        if not dry:
            pace_conv('cv%d' % (NA - 1), len(conv_pending))
        P.barrier()

    iq = 0
    iyc = QC
    iat = QC + CC
    img = 2 * QC + CC
    SCALE = 1.0 / math.sqrt(128.0)

    ln_sqs = {}

    def ln_begin():
        reserved.update((6, 7))

    ln_pool = {'on': False}

    def ln_sq(o):
        sq, sqB = scr()
        act(sq[:, :], tt_[:, o, :], AF.Square, [tB[o]], [sqB])
        if ln_pool['on']:
            if o == 0:
                P.op('pool', lambda e: e.tensor_copy(ln_mean[:, :], tt_[:, 0, :]), [tB[0]], [ln_meanB])
                P.op('pool', lambda e, q=sq: e.tensor_copy(ln_rstd[:, :], q[:, :]), [sqB], [ln_rstdB])
            else:
                P.op('pool', lambda e, o=o: e.tensor_tensor(ln_mean[:, :], ln_mean[:, :], tt_[:, o, :], ALU.add),
                     [tB[o], ln_meanB], [ln_meanB])
                P.op('pool', lambda e, q=sq: e.tensor_tensor(ln_rstd[:, :], ln_rstd[:, :], q[:, :], ALU.add),
                     [sqB, ln_rstdB], [ln_rstdB])
            return
        ln_sqs[o] = (sq, sqB)

    def ln_mm(o):
        if ln_pool['on']:
            if o == DC - 1:
                mm(psum[6][:, 0:TW], ones_f[:, :], ln_mean[:, :], True, True, [constB, ln_meanB], [psB[6]])
                mm(psum[7][:, 0:TW], ones_f[:, :], ln_rstd[:, :], True, True, [constB, ln_rstdB], [psB[7]])
            return
        sq, sqB = ln_sqs.pop(o)
        mm(psum[6][:, 0:TW], ones_f[:, :], tt_[:, o, :], o == 0, o == DC - 1, [constB, tB[o]], [psB[6]])
        mm(psum[7][:, 0:TW], ones_f[:, :], sq[:, :], o == 0, o == DC - 1, [constB, sqB], [psB[7]])

    def ln_finish(m, gcol, bcol, write_bf16):
        bsum, bsumB, bsq, bsqB = psum[6], psB[6], psum[7], psB[7]
        mean, meanB = ln_mean, ln_meanB
        dts(mean[:, :], bsum[:, 0:TW], 1.0 / D, None, ALU.mult, None, [bsumB], [meanB])
        msq, msqB = scr()
        dtt(msq[:, :], mean[:, :], mean[:, :], ALU.mult, [meanB], [msqB])
        var, varB = scr()
        dstt(var[:, :], bsq[:, 0:TW], 1.0 / D, msq[:, :], ALU.mult, ALU.subtract, [bsqB, msqB], [varB])
        reserved.clear()
        rstd, rstdB = ln_rstd, ln_rstdB
        act(rstd[:, :], var[:, :], AF.Ln, [varB, constB], [rstdB], bias=eps_t[:, 0:1], scale=1.0)
        act(rstd[:, :], rstd[:, :], AF.Exp, [rstdB], [rstdB], scale=-0.5)
        fl = pcol(c['c_fl'] + m)
        tmps = {}
        for o in range(DC + 3):
            if o < DC:
                tmp, tmpB = scr()
                tmps[o] = (tmp, tmpB)
                dtt(tmp[:, :], tt_[:, o, :], mean[:, :], ALU.subtract, [tB[o], meanB], [tmpB])
            if 0 <= o - 1 < DC:
                tmp, tmpB = tmps[o - 1]
                dtt(tmp[:, :], tmp[:, :], rstd[:, :], ALU.mult, [tmpB, rstdB], [tmpB])
            if 0 <= o - 2 < DC:
                q_ = o - 2
                tmp, tmpB = tmps.pop(q_)
                act(tt_[:, q_, :], tmp[:, :], AF.Identity, [tmpB, parB], [tB[q_]],
                    bias=pcol(bcol + q_), scale=pcol(gcol + q_))
                if write_bf16:
                    act(hb[:, q_, :], tmp[:, :], AF.Identity, [tmpB, parB], [hbB[q_]],
                        bias=pcol(bcol + q_), scale=pcol(gcol + q_))
            if write_bf16 and 0 <= o - 3 < DC:
                q_ = o - 3
                dts(hb[:, q_, 0:HALO], hb[:, q_, 0:HALO], fl, None, ALU.mult, None, [hbB[q_], parB], [hbB[q_]])

    LAG = 2

    if 'B' in phases:
        for m in range(NT):
            ln_pool['on'] = m >= 1
            xt_v = xT_tiles[m].rearrange("(c p) t -> p c t", p=128)
            if m == 0:
                P.dma('pool', hb[:, :, :], xt_v, 'ld_hb', writes=hbB)
                P.dma('pool', maskt[:, :, :], masks_d[m].rearrange("p (b t) -> p b t", t=TW), 'ld_mask',
                      writes=[maskB])
            P.dma('pool', tt_[:, :, :], xt_v, 'ld_t', writes=tB)

            for s in range(QC // 2):
                col0 = c['oq'] + 256 * s
                wt, wb = wload(('q', s), [(0, 'in', 0, DC, col0, 256)])
                wv = v3(0, DC, 256)(wt)
                for j in range(2):
                    ch = 2 * s + j
                    bank, bB = ps()
                    for kc in range(DC):
                        mm(bank[:, 0:TW], wv[:, kc, j * 128:(j + 1) * 128], hb[:, kc, :], kc == 0, kc == DC - 1,
                           [wb, hbB[kc]], [bB])
                    evac(Rv[:, iq + ch, :], bank[:, 0:TW], [bB], [RB[iq + ch]])

            for s in range(CC // 2):
                col0 = c['ogb'] + 256 * s
                wt, wb = wload(('gb', s), [(0, 'in', 0, DC, col0, 256)])
                wv = v3(0, DC, 256)(wt)
                gbs = []
                for j in range(2):
                    bank, bB = ps()
                    for kc in range(DC):
                        mm(bank[:, 0:TW], wv[:, kc, j * 128:(j + 1) * 128], hb[:, kc, :], kc == 0, kc == DC - 1,
                           [wb, hbB[kc]], [bB])
                    g, gB = scr()
                    act(g[:, :], bank[:, 0:TW], AF.Copy, [bB], [gB])
                    gbs.append((g, gB))
                for j in range(2):
                    i = 2 * s + j
                    cu = c['ou'] + 128 * i
                    cg = c['ogc'] + 128 * i
                    wt, wb = wload(('ug', i), [(0, 'in', 0, DC, cu, 128), (DC * 128, 'in', 0, DC, cg, 128)])
                    wu = v3(0, DC, 128)(wt)
                    wg = v3(DC * 128, DC, 128)(wt)
                    bu, buB = ps()
                    bg, bgB = ps()
                    for kc in range(DC):
                        mm(bu[:, 0:TW], wu[:, kc, :], hb[:, kc, :], kc == 0, kc == DC - 1, [wb, hbB[kc]], [buB])
                    for kc in range(DC):
                        mm(bg[:, 0:TW], wg[:, kc, :], hb[:, kc, :], kc == 0, kc == DC - 1, [wb, hbB[kc]], [bgB])
                    us, usB = scr()
                    act(us[:, :], bu[:, 0:TW], AF.Copy, [buB], [usB])
                    gcu, gcuB = scr()
                    dtt(gcu[:, :], bg[:, 0:TW], us[:, :], ALU.mult, [bgB, usB], [gcuB])
                    cv, cvB = scr()
                    cw = c['c_cm'] + 3 * i
                    dts(cv[:, :], gcu[:, :], pcol(cw + 2), None, ALU.mult, None, [gcuB, parB], [cvB])
                    dstt(cv[:, 1:TW], gcu[:, 0:TW - 1], pcol(cw + 1), cv[:, 1:TW], ALU.mult, ALU.add,
                         [gcuB, cvB, parB], [cvB])
                    dstt(cv[:, 2:TW], gcu[:, 0:TW - 2], pcol(cw + 0), cv[:, 2:TW], ALU.mult, ALU.add,
                         [gcuB, cvB, parB], [cvB])
                    g, gB = gbs[j]
                    dtt(Rv[:, iyc + i, :], cv[:, :], g[:, :], ALU.mult, [cvB, gB], [RB[iyc + i]])

            groups = []
            for g in range(m):
                if g == 0:
                    groups.append((1, 4, False, 0))
                    groups.append((5, 3, False, 0))
                else:
                    groups.append((8 * g, 4, False, 0))
                    groups.append((8 * g + 4, 4, False, 0))
            groups.append((8 * m, 5, True, 0))
            groups.append((8 * m + 5, 4, True, 5))
            ACC = [(psum[k], psB[k]) for k in range(3)]
            SBK = [3, 4, 5, 6, 7]
            LOOK = 4
            seq = []
            for h in range(H):
                for cc in range(2):
                    for gi, (b0, nb, win, wb0) in enumerate(groups):
                        for bl in range(nb):
                            seq.append((h, cc, gi, bl))
            nper = sum(nb for (_, nb, _, _) in groups)
            loaded = {}
            info = {}
            deferred = []

            def emitS(n):
                h, cc, gi, bl = seq[n]
                b0, nb, win, wb0 = groups[gi]
                lk = (h, cc, gi)
                if lk not in loaded:
                    si = st['kv'] % NKV
                    st['kv'] += 1
                    P.dma('sp', kt_sl[si][:, 0:nb * 128], KT[2 * h + cc, :, b0 * 128:(b0 + nb) * 128],
                          'ld_kv%d' % si, writes=[kvB[si]])
                    P.dma('sp', v_sl[si][:, 0:nb, :],
                          VS[b0 * 128:(b0 + nb) * 128, 256 * h:256 * h + 256].rearrange("(b p) v -> p b v", p=128),
                          'ld_kv%d' % si, writes=[kvB[si]])
                    loaded[lk] = si
                si = loaded[lk]
                sb = SBK[st['sb'] % 5]
                st['sb'] += 1
                pi = st['pt'] % NPT
                st['pt'] += 1
                qch = iq + 2 * h + cc
                mm(psum[sb][:, 0:TW], kt_sl[si][:, bl * 128:(bl + 1) * 128], Rv[:, qch, :],
                   True, True, [kvB[si], RB[qch]], [psB[sb]])
                act(pt[pi], psum[sb][:, 0:TW], AF.Exp, [psB[sb]], [ptB[pi]], scale=SCALE)
                if win:
                    dtt(pt[pi], pt[pi], maskt[:, wb0 + bl, :], ALU.mult, [ptB[pi], maskB], [ptB[pi]])
                info[n] = (si, pi)

            def emitPV(n):
                h, cc, gi, bl = seq[n]
                si, pi = info.pop(n)
                idx = n % nper
                fs = idx == 0
                ls = idx == nper - 1
                for k in range(2):
                    mm(ACC[k][0][:, 0:TW], v_sl[si][:, bl, 128 * k:128 * k + 128], pt[pi],
                       fs, ls, [kvB[si], ptB[pi]], [ACC[k][1]])
                mm(ACC[2][0][:, 0:TW], ones_b[:, :], pt[pi], fs, ls, [constB, ptB[pi]], [ACC[2][1]])

            def epi0(h):
                rr, rrB = scr()
                P.op('dve', lambda e, o=rr, i=ACC[2][0]: e.reciprocal(o[:, :], i[:, 0:TW]), [ACC[2][1]], [rrB])
                for k in range(2):
                    dtt(o1buf[k][:, :], ACC[k][0][:, 0:TW], rr[:, :], ALU.mult, [ACC[k][1], rrB], [o1B[k]])

            def epi1(h):
                rr, rrB = scr()
                P.op('dve', lambda e, o=rr, i=ACC[2][0]: e.reciprocal(o[:, :], i[:, 0:TW]), [ACC[2][1]], [rrB])
                os_ = []
                sqs = []
                for k in range(2):
                    o_, oB = scr()
                    dtt(o_[:, :], ACC[k][0][:, 0:TW], rr[:, :], ALU.mult, [ACC[k][1], rrB], [oB])
                    dstt(o_[:, :], o_[:, :], neglam, o1buf[k][:, :], ALU.mult, ALU.add, [oB, o1B[k], lamB], [oB])
                    sq, sqB = scr()
                    act(sq[:, :], o_[:, :], AF.Square, [oB], [sqB])
                    os_.append((o_, oB))
                    sqs.append((sq, sqB))

                def partB(h=h, os_=os_, sqs=sqs):
                    mb = SBK[st['sb'] % 5]
                    st['sb'] += 1
                    for k in range(2):
                        mm(psum[mb][:, 0:TW], ones_f[:, :], sqs[k][0][:, :], k == 0, k == 1, [constB, sqs[k][1]],
                           [psB[mb]])
                    rstd, rstdB = scr()
                    act(rstd[:, :], psum[mb][:, 0:TW], AF.Ln, [psB[mb], constB], [rstdB], bias=eps_t[:, 0:1],
                        scale=1.0 / 256.0)
                    act(rstd[:, :], rstd[:, :], AF.Exp, [rstdB], [rstdB], scale=-0.5)
                    for k in range(2):
                        ai = iat + 2 * h + k
                        dstt(Rv[:, ai, :], os_[k][0][:, :], lamt[:, 6 + k:7 + k], rstd[:, :], ALU.mult, ALU.mult,
                             [os_[k][1], rstdB, lamB], [RB[ai]])
                deferred.append([4, partB])

            for n in range(min(LOOK, len(seq))):
                emitS(n)
            for n in range(len(seq)):
                if n + LOOK < len(seq):
                    emitS(n + LOOK)
                emitPV(n)
                for d in deferred:
                    d[0] -= 1
                while deferred and deferred[0][0] <= 0:
                    deferred.pop(0)[1]()
                h, cc, gi, bl = seq[n]
                if n % nper == nper - 1:
                    if cc == 0:
                        epi0(h)
                    else:
                        epi1(h)
            while deferred:
                deferred.pop(0)[1]()

            if m + 1 < NT:
                P.dma('pool', maskt[:, :, :], masks_d[m + 1].rearrange("p (b t) -> p b t", t=TW), 'ld_mask',
                      writes=[maskB])
            for s in range(DC // 2):
                for j in range(2):
                    o = 2 * s + j
                    ca = c['oga'] + 128 * o
                    cg = c['ogcv'] + 128 * o
                    wt, wb = wload(('g', o), [(0, 'in', 0, DC, ca, 128), (DC * 128, 'in', 0, DC, cg, 128)])
                    wtA, wbA = wload(('ac', o), [(0, 'ao', 0, QC, 128 * o, 128), (QC * 128, 'co', 0, CC, 128 * o, 128)])
                    wa = v3(0, QC, 128)(wtA)
                    wc = v3(QC * 128, CC, 128)(wtA)
                    wga = v3(0, DC, 128)(wt)
                    wgc = v3(DC * 128, DC, 128)(wt)
                    bga, bgaB = ps()
                    bgc, bgcB = ps()
                    bya, byaB = ps()
                    byc, bycB = ps()
                    for kc in range(DC):
                        mm(bga[:, 0:TW], wga[:, kc, :], hb[:, kc, :], kc == 0, kc == DC - 1, [wb, hbB[kc]], [bgaB])
                    for kc in range(DC):
                        mm(bgc[:, 0:TW], wgc[:, kc, :], hb[:, kc, :], kc == 0, kc == DC - 1, [wb, hbB[kc]], [bgcB])
                    for kc in range(QC):
                        mm(bya[:, 0:TW], wa[:, kc, :], Rv[:, iat + kc, :], kc == 0,
                           kc == QC - 1, [wbA, RB[iat + kc]], [byaB])
                    for kc in range(CC):
                        mm(byc[:, 0:TW], wc[:, kc, :], Rv[:, iyc + kc, :], kc == 0,
                           kc == CC - 1, [wbA, RB[iyc + kc]], [bycB])
                    sa, saB = scr()
                    act(sa[:, :], bga[:, 0:TW], AF.Sigmoid, [bgaB], [saB])
                    sc_, scB = scr()
                    act(sc_[:, :], bgc[:, 0:TW], AF.Sigmoid, [bgcB], [scB])
                    m1, m1B = scr()
                    dtt(m1[:, :], bya[:, 0:TW], sa[:, :], ALU.mult, [byaB, saB], [m1B])
                    m2, m2B = scr()
                    dtt(m2[:, :], byc[:, 0:TW], sc_[:, :], ALU.mult, [bycB, scB], [m2B])
                    dtt(Rv[:, img + o, :], m1[:, :], m2[:, :], ALU.add, [m1B, m2B], [RB[img + o]])

            ln_begin()
            for s in range(DC // 2):
                c2 = 256 * s
                wt, wb = wload(('wo', s), [(0, 'o', 0, DC, c2, 256)])
                wv = v3(0, DC, 256)(wt)
                for j in range(2):
                    o = 2 * s + j
                    bank, bB = ps()
                    for kc in range(DC):
                        mm(bank[:, 0:TW], wv[:, kc, j * 128:(j + 1) * 128], Rv[:, img + kc, :], kc == 0,
                           kc == DC - 1, [wb, RB[img + kc]], [bB])
                    dstt(tt_[:, o, :], tt_[:, o, :], ALPHA, bank[:, 0:TW], ALU.mult, ALU.add, [tB[o], bB], [tB[o]])
                    ln_sq(o)
                    if o - LAG >= 0:
                        ln_mm(o - LAG)

            for o in range(max(0, DC - LAG), DC):
                ln_mm(o)
            ln_finish(m, c['c_ln'], c['c_ln'] + DC, True)

            for i in range(FC):
                cg = 128 * i
                cvv = DFF + 128 * i
                wt, wb = wload(('up', i), [(0, 'up', 0, DC, cg, 128), (DC * 128, 'up', 0, DC, cvv, 128)])
                wg = v3(0, DC, 128)(wt)
                wvv = v3(DC * 128, DC, 128)(wt)
                bg, bgB = ps()
                bv, bvB = ps()
                for kc in range(DC):
                    mm(bg[:, 0:TW], wg[:, kc, :], hb[:, kc, :], kc == 0, kc == DC - 1, [wb, hbB[kc]], [bgB])
                for kc in range(DC):
                    mm(bv[:, 0:TW], wvv[:, kc, :], hb[:, kc, :], kc == 0, kc == DC - 1, [wb, hbB[kc]], [bvB])
                res = []
                for (bk, bkB, ci) in ((bg, bgB, i), (bv, bvB, FC + i)):
                    cw = c['c_fc'] + 3 * ci
                    cvt, cvtB = scr()
                    act(cvt[:, :], bk[:, 0:TW], AF.Identity, [bkB, parB], [cvtB], scale=pcol(cw + 2))
                    dstt(cvt[:, 1:TW], bk[:, 0:TW - 1], pcol(cw + 1), cvt[:, 1:TW], ALU.mult, ALU.add,
                         [bkB, cvtB, parB], [cvtB])
                    dstt(cvt[:, 2:TW], bk[:, 0:TW - 2], pcol(cw + 0), cvt[:, 2:TW], ALU.mult, ALU.add,
                         [bkB, cvtB, parB], [cvtB])
                    res.append((cvt, cvtB))
                sg, sgB = scr()
                act(sg[:, :], res[0][0][:, :], AF.Silu, [res[0][1]], [sgB])
                dtt(Rv[:, i, :], sg[:, :], res[1][0][:, :], ALU.mult, [sgB, res[1][1]], [RB[i]])

            if m + 1 < NT:
                P.dma('pool', hb[:, :, :], xT_tiles[m + 1].rearrange("(c p) t -> p c t", p=128), 'ld_hb', writes=hbB)

            ln_begin()
            K1 = (FC + 1) // 2
            K2 = FC - K1
            for o in range(DC):
                c2 = 128 * o
                wt1, wb1 = wload(('dn', o, 0), [(0, 'dn', 0, K1, c2, 128)])
                wt2, wb2 = wload(('dn', o, 1), [(0, 'dn', K1, K2, c2, 128)])
                w1 = v3(0, K1, 128)(wt1)
                w2 = v3(0, K2, 128)(wt2)
                bank, bB = ps()
                for kc in range(FC):
                    if kc < K1:
                        mm(bank[:, 0:TW], w1[:, kc, :], Rv[:, kc, :], kc == 0, kc == FC - 1, [wb1, RB[kc]], [bB])
                    else:
                        mm(bank[:, 0:TW], w2[:, kc - K1, :], Rv[:, kc, :], kc == 0, kc == FC - 1, [wb2, RB[kc]], [bB])
                dstt(tt_[:, o, :], tt_[:, o, :], ALPHA, bank[:, 0:TW], ALU.mult, ALU.add, [tB[o], bB], [tB[o]])
                ln_sq(o)
                if o - LAG >= 0:
                    ln_mm(o - LAG)

            for o in range(max(0, DC - LAG), DC):
                ln_mm(o)
            ln_finish(m, c['c_ln'] + 2 * DC, c['c_ln'] + 3 * DC, False)
            tok = P.dma('pool', yT[m].rearrange("(c p) t -> p c t", p=128), tt_[:, :, HALO:TW], 'st_y', reads=tB)
        P.final_wait('pool', [tok])
    else:
        P.final_wait('pool', [('D', sk, v) for sk, v in P.dmacnt.items()])

    if dry:
        return rec
    P.emit()
    return nc, P


def build(cfg):
    plan = _build(cfg, None)
    nc, P = _build(cfg, plan)
    P.plan = plan
    return nc, P


def host_inputs(cfg, plan, x, w_in, lambda_q1, lambda_k1, lambda_q2, lambda_k2, subln_g, conv_mix_w,
                w_attn_out, w_conv_out, w_o, ln1_g, ln1_b, w_up, ffn_conv_w, w_down, ln2_g, ln2_b):
    c = derive(cfg)
    DC, H, CC, FC, NT = c['DC'], c['H'], c['CC'], c['FC'], c['NT']
    D, SEQ = c['D'], c['SEQ']
    x = np.asarray(x, dtype=np.float32)
    B = x.shape[0]
    assert B == 2 and x.shape[1] == SEQ and x.shape[2] == D
    f = lambda a: np.ascontiguousarray(np.asarray(a, dtype=np.float32))
    W = {'in': f(w_in[0]), 'ao': f(w_attn_out[0]), 'co': f(w_conv_out[0]),
         'o': f(w_o[0]), 'up': f(w_up[0]), 'dn': f(w_down[0])}
    WSZ = max(DC * 256, (c['QC'] + CC) * 256, ((FC + 1) // 2) * 128)
    wsrc = np.zeros((len(plan), 128, WSZ), np.float32)
    for k, (key, parts) in enumerate(plan.items()):
        for (doff, sname, kc0, nk, col0, ncol) in parts:
            blk = W[sname][kc0 * 128:(kc0 + nk) * 128, col0:col0 + ncol]
            wsrc[k, :, doff:doff + nk * ncol] = blk.reshape(nk, 128, ncol).transpose(1, 0, 2).reshape(128, nk * ncol)
    shared = {"wsrc": wsrc}
    par = np.zeros((128, c['NP']), np.float32)
    lamv = np.concatenate([f(lambda_q1[0]), f(lambda_k1[0]), f(lambda_q2[0]), f(lambda_k2[0])])
    par[:, 0:512] = lamv[None, :]
    par[:, c['c_sg']:c['c_sg'] + 2] = f(subln_g[0]).reshape(2, 128).T
    par[:, c['c_cm']:c['c_cm'] + CC * 3] = f(conv_mix_w[0]).reshape(3, CC, 128).transpose(2, 1, 0).reshape(128, CC * 3)
    par[:, c['c_fc']:c['c_fc'] + 2 * FC * 3] = f(ffn_conv_w[0]).reshape(3, 2 * FC, 128).transpose(2, 1, 0).reshape(128, 2 * FC * 3)
    lnp = np.stack([f(ln1_g[0]), f(ln1_b[0]), f(ln2_g[0]), f(ln2_b[0])])
    par[:, c['c_ln']:c['c_ln'] + 4 * DC] = lnp.reshape(4, DC, 128).transpose(2, 0, 1).reshape(128, 4 * DC)
    xT = [np.ascontiguousarray(x[b].T) for b in range(B)]
    in_maps = []
    kk = np.arange(128)[:, None, None]
    wb = np.arange(9)[None, :, None]
    qi = np.arange(TW)[None, None, :]
    for core in range(8):
        b, j = core // 4, core % 4
        tiles = np.zeros((NT, D, TW), np.float32)
        masks = np.zeros((NT, 128, 9, TW), np.float32)
        p_ = par.copy()
        for m in range(NT):
            s = (4 * m + j) * OWN
            lo = s - HALO
            if lo >= 0:
                tiles[m] = xT[b][:, lo:s + OWN]
                p_[:, c['c_fl'] + m] = 1.0
            else:
                tiles[m][:, HALO:] = xT[b][:, 0:OWN]
                p_[:, c['c_fl'] + m] = 0.0
            key = 128 * (8 * m + wb) + kk - 128
            pos = s - HALO + qi
            vis = (key >= 0) & (key <= pos)
            padq = (pos < 0) & (key < 0)
            masks[m] = (vis | padq).astype(np.float32)
        mp = dict(shared)
        mp["xT_full"] = xT[b]
        mp["xT_tiles"] = tiles
        mp["params"] = p_
        mp["masks"] = masks.reshape(NT, 128, 9 * TW).astype(ml_dtypes.bfloat16)
        in_maps.append(mp)
    return in_maps


def assemble(cfg, results):
    c = derive(cfg)
    NT, D, SEQ = c['NT'], c['D'], c['SEQ']
    out = np.zeros((2, SEQ, D), np.float32)
    for core in range(8):
        b, j = core // 4, core % 4
        y = np.asarray(results[core]["yT"])
        for m in range(NT):
            s = (4 * m + j) * OWN
            out[b, s:s + OWN, :] = y[m].T
    return out


def kernel(**inputs):
    cfg = REAL_CFG
    nc, P = build(cfg)
    in_maps = host_inputs(cfg, P.plan, **inputs)
    res = run_bass_kernel_spmd(nc, in_maps, core_ids=list(range(8)))
    return assemble(cfg, res.results)
```

```python
import math
import numpy as np
import ml_dtypes
import concourse.bass as bass
import concourse.mybir as mybir
from concourse.bass_utils import run_bass_kernel_spmd

F32 = mybir.dt.float32
BF16 = mybir.dt.bfloat16
AF = mybir.ActivationFunctionType
ALU = mybir.AluOpType
AX = mybir.AxisListType

TW = 260
HALO = 4
OWN = 256
LN_EPS = 1e-5
RMS_EPS = 1e-5
ALPHA = 2.0 ** 0.25
LAM_INIT = 0.8 - 0.6 * math.exp(-0.3 * 0)

REAL_CFG = dict(DC=32, H=8, CC=16, FC=86, NT=8)


def derive(cfg):
    c = dict(cfg)
    DC, H, CC, FC, NT = cfg['DC'], cfg['H'], cfg['CC'], cfg['FC'], cfg['NT']
    c['D'] = DC * 128
    c['SEQ'] = 4 * NT * OWN
    c['QC'] = 2 * H
    c['VW'] = H * 256
    c['CW'] = CC * 128
    c['DFF'] = FC * 128
    c['oq'] = 0
    c['ok'] = H * 256
    c['ov'] = 2 * H * 256
    c['ou'] = 3 * H * 256
    c['ogb'] = c['ou'] + c['CW']
    c['ogc'] = c['ogb'] + c['CW']
    c['oga'] = c['ogc'] + c['CW']
    c['ogcv'] = c['oga'] + c['D']
    c['INW'] = c['ogcv'] + c['D']
    c['TA'] = 1024
    c['NA'] = c['SEQ'] // c['TA']
    c['SP'] = c['SEQ'] + 128
    c['c_lam'] = 0
    c['c_sg'] = 512
    c['c_cm'] = 514
    c['c_fc'] = c['c_cm'] + CC * 3
    c['c_ln'] = c['c_fc'] + 2 * FC * 3
    c['c_fl'] = c['c_ln'] + 4 * DC
    c['NP'] = c['c_fl'] + NT
    return c


class Buf:
    __slots__ = ('name', 'w', 'r')

    def __init__(self, name):
        self.name = name
        self.w = {}
        self.r = {}


ENGS = ('pe', 'act', 'dve', 'pool', 'sp')


class Prog:
    def __init__(self, nc):
        self.nc = nc
        self.ops = {e: [] for e in ENGS}
        self.waited = {e: {} for e in ENGS}
        self.pending = {e: [] for e in ENGS}
        self.dmacnt = {}
        self.final = {e: [] for e in ENGS}

    def _need(self, eng, tok, waits):
        if tok[0] == 'E':
            _, src, idx = tok
            key = ('E', src)
            if self.waited[eng].get(key, -1) >= idx:
                return
            self.waited[eng][key] = idx
            self.ops[src][idx]['ms'] = True
            waits.append(tok)
        else:
            _, sk, val = tok
            key = ('D', sk)
            if self.waited[eng].get(key, -1) >= val:
                return
            self.waited[eng][key] = val
            waits.append(tok)

    def _deps(self, eng, mykey, reads, writes):
        waits = []
        for tok in self.pending[eng]:
            self._need(eng, tok, waits)
        self.pending[eng] = []
        for b in reads:
            for k, tok in b.w.items():
                if k == mykey and eng == 'pe':
                    continue
                self._need(eng, tok, waits)
        for b in writes:
            for k, tok in b.r.items():
                if k != mykey:
                    self._need(eng, tok, waits)
            for k, tok in b.w.items():
                if k != mykey:
                    self._need(eng, tok, waits)
        return waits

    def op(self, eng, fn, reads=(), writes=()):
        waits = self._deps(eng, eng, reads, writes)
        idx = len(self.ops[eng])
        tok = ('E', eng, idx)
        self.ops[eng].append({'fn': fn, 'waits': waits, 'ms': False, 'dma': None})
        for b in reads:
            b.r[eng] = tok
        for b in writes:
            b.w = {eng: tok}
            b.r = {}
        return tok

    def dma(self, q, out, in_, sk, reads=(), writes=()):
        mykey = ('D', sk)
        waits = self._deps(q, mykey, reads, writes)
        self.dmacnt[sk] = self.dmacnt.get(sk, 0) + 16
        tok = ('D', sk, self.dmacnt[sk])
        self.ops[q].append({'fn': (lambda e, o=out, i=in_: e.dma_start(out=o, in_=i)),
                            'waits': waits, 'ms': False, 'dma': sk})
        for b in reads:
            b.r[mykey] = tok
        for b in writes:
            b.w = {mykey: tok}
            b.r = {}
        return tok

    def barrier(self, exclude=()):
        toks = []
        for e in ENGS:
            if self.ops[e]:
                for idx in range(len(self.ops[e]) - 1, -1, -1):
                    if self.ops[e][idx]['dma'] is None:
                        toks.append(('E', e, idx))
                        break
        for sk, v in self.dmacnt.items():
            if sk not in exclude:
                toks.append(('D', sk, v))
        for e in ENGS:
            self.pending[e] = list(toks)

    def final_wait(self, eng, toks):
        waits = []
        for t in toks:
            self._need(eng, t, waits)
        self.final[eng].extend(waits)

    def emit(self):
        nc = self.nc
        semval = {}
        esem = {}
        for e in ENGS:
            cnt = 0
            vals = []
            for o in self.ops[e]:
                if o['ms']:
                    cnt += 1
                vals.append(cnt)
            semval[e] = vals
            esem[e] = nc.alloc_semaphore('es_' + e)
        dsem = {sk: nc.alloc_semaphore('ds_' + sk) for sk in self.dmacnt}
        self.maxvals = {e: (semval[e][-1] if semval[e] else 0) for e in ENGS}

        def run(ename, eh):
            def dowait(tok):
                if tok[0] == 'E':
                    eh.wait_ge(esem[tok[1]], semval[tok[1]][tok[2]])
                else:
                    eh.wait_ge(dsem[tok[1]], tok[2])
            for o in self.ops[ename]:
                for tok in o['waits']:
                    dowait(tok)
                ins = o['fn'](eh)
                if o['dma'] is not None:
                    ins.then_inc(dsem[o['dma']], 16)
                elif o['ms']:
                    ins.then_inc(esem[ename], 1)
            for tok in self.final[ename]:
                dowait(tok)

        with nc.Block() as block:
            @block.tensor
            def _(e):
                run('pe', e)

            @block.scalar
            def _(e):
                run('act', e)

            @block.vector
            def _(e):
                run('dve', e)

            @block.gpsimd
            def _(e):
                run('pool', e)

            @block.sync
            def _(e):
                run('sp', e)


def _build(cfg, plan, phases=('A', 'B')):
    c = derive(cfg)
    DC, H, CC, FC, NT = c['DC'], c['H'], c['CC'], c['FC'], c['NT']
    D, SEQ, QC, VW, CW, DFF = c['D'], c['SEQ'], c['QC'], c['VW'], c['CW'], c['DFF']
    TA, NA, SP, NPAR = c['TA'], c['NA'], c['SP'], c['NP']
    assert CC % 2 == 0 and DC % 2 == 0 and QC % 2 == 0

    nc = bass.Bass("TRN2", target_bir_lowering=False)
    P = Prog(nc)

    xT_full = nc.dram_tensor("xT_full", [D, SEQ], F32, kind="ExternalInput").ap()
    xT_tiles = nc.dram_tensor("xT_tiles", [NT, D, TW], F32, kind="ExternalInput").ap()
    params_d = nc.dram_tensor("params", [128, NPAR], F32, kind="ExternalInput").ap()
    masks_d = nc.dram_tensor("masks", [NT, 128, 9 * TW], BF16, kind="ExternalInput").ap()
    yT = nc.dram_tensor("yT", [NT, D, OWN], F32, kind="ExternalOutput").ap()
    KT = nc.dram_tensor("kt_scr", [QC, 128, SP], BF16).ap()
    VS = nc.dram_tensor("v_scr", [SP, VW], BF16).ap()

    xT_full_v = xT_full.rearrange("(c p) t -> p c t", p=128)

    RC = max(FC, 2 * QC + CC + DC)
    KVSZ = 2 * 9 * 128 + 9 * 256
    NPT = 6
    BIGSZ = max(DC * TA, RC * TW + 2 * KVSZ + NPT * TW)
    big = nc.alloc_sbuf_tensor("big", [128, BIGSZ], BF16)
    hb_flat = nc.alloc_sbuf_tensor("hb", [128, DC * TW], BF16)
    hb = hb_flat[:, :].rearrange("p (c t) -> p c t", t=TW)
    NST = 6
    assert DC * TW >= NST * (512 + 256)
    tt_ = nc.alloc_sbuf_tensor("tres", [128, DC, TW], F32)
    WSZ = max(DC * 256, (QC + CC) * 256, ((FC + 1) // 2) * 128)
    NS = 4
    wsl = [nc.alloc_sbuf_tensor("wsl%d" % i, [128, WSZ], BF16) for i in range(NS)]
    NSC = 10
    scrt = [nc.alloc_sbuf_tensor("scr%d" % i, [128, TW], F32) for i in range(NSC)]
    maskt = nc.alloc_sbuf_tensor("maskt", [128, 9, TW], BF16)
    o1buf = [nc.alloc_sbuf_tensor("o1buf%d" % i, [128, TW], F32) for i in range(2)]
    o1B = [Buf('o1b%d' % i) for i in range(2)]
    ln_mean = nc.alloc_sbuf_tensor("ln_mean", [128, TW], F32)
    ln_rstd = nc.alloc_sbuf_tensor("ln_rstd", [128, TW], F32)
    ln_meanB = Buf('ln_mean')
    ln_rstdB = Buf('ln_rstd')
    par = nc.alloc_sbuf_tensor("par", [128, NPAR], F32)
    lamt = nc.alloc_sbuf_tensor("lamt", [128, 8], F32)
    ones_f = nc.alloc_sbuf_tensor("ones_f", [128, 128], F32)
    ones_b = nc.alloc_sbuf_tensor("ones_b", [128, 128], BF16)
    zero_b = nc.alloc_sbuf_tensor("zero_b", [128, 256], BF16)
    eps_t = nc.alloc_sbuf_tensor("eps_t", [128, 1], F32)
    kst = [hb_flat[:, 512 * i:512 * (i + 1)] for i in range(NST)]
    vst = [hb_flat[:, 512 * NST + 256 * i:512 * NST + 256 * (i + 1)] for i in range(NST)]
    psum = [nc.alloc_psum_tensor("ps%d" % i, [128, 512], F32) for i in range(8)]

    xA = big[:, 0:DC * TA].rearrange("p (c t) -> p c t", t=TA)
    Rv = big[:, 0:RC * TW].rearrange("p (c t) -> p c t", t=TW)
    kvoff = RC * TW
    NKV = 4
    KVS = 5 * 128 + 5 * 256
    assert NKV * KVS <= 2 * KVSZ
    kt_sl = [big[:, kvoff + i * KVS: kvoff + i * KVS + 640] for i in range(NKV)]
    v_sl = [big[:, kvoff + i * KVS + 640: kvoff + (i + 1) * KVS].rearrange("p (b v) -> p b v", v=256) for i in range(NKV)]
    ptoff = kvoff + 2 * KVSZ
    pt = [big[:, ptoff + i * TW: ptoff + (i + 1) * TW] for i in range(NPT)]

    xAB = Buf('xA')
    RB = [Buf('R%d' % i) for i in range(RC)]
    hbB = [Buf('hb%d' % i) for i in range(DC)]
    tB = [Buf('t%d' % i) for i in range(DC)]
    wB = [Buf('w%d' % i) for i in range(NS)]
    scrB = [Buf('scr%d' % i) for i in range(NSC)]
    maskB = Buf('mask')
    parB = Buf('par')
    lamB = Buf('lam')
    constB = Buf('const')
    kstB = [Buf('kst%d' % i) for i in range(NST)]
    vstB = [Buf('vst%d' % i) for i in range(NST)]
    psB = [Buf('ps%d' % i) for i in range(8)]
    kvB = [Buf('kv%d' % i) for i in range(4)]
    ptB = [Buf('pt%d' % i) for i in range(NPT)]
    ktdB = Buf('ktd')

    st = {'ps': 0, 'scr': 0, 'w': 0, 'kv': 0, 'pt': 0, 'ev': 0, 'kst': 0, 'vst': 0, 'sb': 0}

    reserved = set()

    def ps():
        while True:
            i = st['ps'] % 8
            st['ps'] += 1
            if i not in reserved:
                return psum[i], psB[i]

    def scr():
        i = st['scr'] % NSC
        st['scr'] += 1
        return scrt[i], scrB[i]

    dry = plan is None
    rec = {}
    if not dry:
        keys = list(plan.keys())
        slabidx = {k: i for i, k in enumerate(keys)}
        wsrc = nc.dram_tensor("wsrc", [len(keys), 128, WSZ], F32, kind="ExternalInput").ap()
        SPT = 100
        wscr_t = [nc.dram_tensor("wscr%d" % g, [min(SPT, len(keys) - g * SPT), 128, WSZ], BF16).ap()
                  for g in range((len(keys) + SPT - 1) // SPT)]

        class _W:
            def __getitem__(self, k):
                return wscr_t[k // SPT][k % SPT]
        wscr = _W()
        batch_of = {}
        convB = {}
        conv_pending = [k for k in keys if k[0] not in ('K', 'V', 'up')]

    def emit_conv_key(key, bname):
        if bname not in convB:
            convB[bname] = Buf(bname)
        k = slabidx[key]
        n = max(doff + nk * ncol for (doff, sname, kc0, nk, col0, ncol) in plan[key])
        P.dma('pool', wscr[k][:, 0:n], wsrc[k][:, 0:n], bname, writes=[convB[bname]])
        batch_of[key] = bname

    def pace_conv(bname, cnt=1):
        for _ in range(cnt):
            if conv_pending:
                key = conv_pending.pop(0)
                emit_conv_key(key, 'cvDN' if key[0] == 'dn' else bname)

    def wload(key, parts):
        i = st['w'] % NS
        st['w'] += 1
        if dry:
            rec.setdefault(key, parts)
        else:
            n = max(doff + nk * ncol for (doff, sname, kc0, nk, col0, ncol) in parts)
            k = slabidx[key]
            if key not in batch_of:
                bname = 'cvF' + key[0]
                if bname not in convB:
                    convB[bname] = Buf(bname)
                P.dma('pool', wsl[i][:, 0:n], wsrc[k][:, 0:n], 'wc%d' % i, writes=[wB[i]])
                P.dma('sp', wscr[k][:, 0:n], wsl[i][:, 0:n], bname, reads=[wB[i]], writes=[convB[bname]])
                batch_of[key] = bname
            else:
                P.dma('sp', wsl[i][:, 0:n], wscr[k][:, 0:n], 'w%d' % i, reads=[convB[batch_of[key]]],
                      writes=[wB[i]])
        return wsl[i], wB[i]

    def v3(off, nk, ncol):
        return lambda t: t[:, off:off + nk * ncol].rearrange("p (c n) -> p c n", n=ncol)

    def mm(out, lhsT, rhs, start, stop, reads, writes):
        P.op('pe', lambda e, o=out, l=lhsT, r=rhs, s=start, p=stop: e.matmul(o, l, r, start=s, stop=p),
             reads, writes)

    def act(out, in_, func, reads, writes, bias=None, scale=None):
        kw = {}
        if bias is not None:
            kw['bias'] = bias
        if scale is not None:
            kw['scale'] = scale
        P.op('act', lambda e, o=out, i=in_, f=func, k=kw: e.activation(o, i, f, **k), reads, writes)

    def dtt(out, in0, in1, op, reads, writes):
        P.op('dve', lambda e, o=out, a=in0, b=in1, p=op: e.tensor_tensor(o, a, b, p), reads, writes)

    def dts(out, in0, s1, s2, op0, op1, reads, writes):
        if op1 is None:
            P.op('dve', lambda e, o=out, a=in0, x=s1, p0=op0: e.tensor_scalar(o, a, x, None, p0), reads, writes)
        else:
            P.op('dve', lambda e, o=out, a=in0, x=s1, y=s2, p0=op0, p1=op1: e.tensor_scalar(o, a, x, y, p0, p1),
                 reads, writes)

    def dstt(out, in0, scalar, in1, op0, op1, reads, writes):
        P.op('dve', lambda e, o=out, a=in0, s=scalar, b=in1, p0=op0, p1=op1:
             e.scalar_tensor_tensor(o, a, s, b, p0, p1), reads, writes)

    def evac(out, in_, reads, writes):
        st['ev'] += 1
        if st['ev'] % 2:
            act(out, in_, AF.Copy, reads, writes)
        else:
            P.op('dve', lambda e, o=out, i=in_: e.tensor_copy(o, i), reads, writes)

    def pcol(col):
        return par[:, col:col + 1]

    P.op('dve', lambda e: e.memset(ones_f[:, :], 1.0), (), [constB])
    P.op('dve', lambda e: e.memset(ones_b[:, :], 1.0), (), [constB])
    P.op('dve', lambda e: e.memset(zero_b[:, :], 0.0), (), [constB])
    assert LN_EPS == RMS_EPS
    P.op('dve', lambda e: e.memset(eps_t[:, :], LN_EPS), (), [constB])
    P.dma('pool', par[:, :], params_d, 'ld_par', writes=[parB])
    s0, s0B = scr()
    for k in range(2):
        dtt(s0[:, 0:128], par[:, 256 * k:256 * k + 128], par[:, 256 * k + 128:256 * k + 256], ALU.mult,
            [parB], [s0B])
        P.op('dve', lambda e, k=k: e.reduce_sum(lamt[:, k:k + 1], s0[:, 0:128], AX.X), [s0B], [lamB])
    act(lamt[:, 2:4], lamt[:, 0:2], AF.Exp, [lamB], [lamB])
    dtt(lamt[:, 4:5], lamt[:, 2:3], lamt[:, 3:4], ALU.subtract, [lamB], [lamB])
    dts(lamt[:, 5:6], lamt[:, 4:5], LAM_INIT, -1.0, ALU.add, ALU.mult, [lamB], [lamB])
    dts(lamt[:, 6:8], par[:, c['c_sg']:c['c_sg'] + 2], 1.0 - LAM_INIT, None, ALU.mult, None, [parB, lamB], [lamB])
    neglam = lamt[:, 5:6]

    if 'A' in phases:
        for ch in range(QC):
            P.dma('pool', KT[ch, :, 0:128], zero_b[:, 0:128], 'st_pad', reads=[constB])
        for s in range(VW // 256):
            P.dma('pool', VS[0:128, 256 * s:256 * s + 256], zero_b[:, :], 'st_pad', reads=[constB])
        for a in range(NA):
            g8 = max(1, DC // 4)
            for c0 in range(0, DC, g8):
                P.dma('pool', xA[:, c0:c0 + g8, :], xT_full_v[:, c0:c0 + g8, a * TA:(a + 1) * TA], 'ld_xa',
                      writes=[xAB])
            if not dry and a == 0:
                for key in keys:
                    if key[0] in ('K', 'V'):
                        emit_conv_key(key, 'cvA%s%d' % (key[0], key[1]))
                nstores = NA * (QC * (TA // 512) + (VW // 256) * (TA // 128))
                n_dn = sum(1 for k in conv_pending if k[0] == 'dn')
                n_early = len(conv_pending) - n_dn
                cut = (nstores * 73) // 100 if n_dn else nstores
                sched = [(i * cut) // max(1, n_early) for i in range(n_early)]
                sched += [cut + (i * (nstores - cut)) // max(1, n_dn) for i in range(n_dn)]
            for s in range(QC // 2):
                col0 = c['ok'] + 256 * s
                wt, wb = wload(('K', s), [(0, 'in', 0, DC, col0, 256)])
                wv = v3(0, DC, 256)(wt)
                for j in range(2):
                    ch = 2 * s + j
                    for t5 in range(TA // 512):
                        bank, bB = ps()
                        for kc in range(DC):
                            mm(bank[:, 0:512], wv[:, kc, j * 128:(j + 1) * 128], xA[:, kc, t5 * 512:(t5 + 1) * 512],
                               kc == 0, kc == DC - 1, [wb, xAB], [bB])
                        n = st['kst'] % NST
                        st['kst'] += 1
                        evac(kst[n][:, :], bank[:, 0:512], [bB], [kstB[n]])
                        off = 128 + a * TA + t5 * 512
                        P.dma('pool', KT[ch, :, off:off + 512], kst[n][:, :], 'st_k%d' % n, reads=[kstB[n]])
                        st['nst'] = st.get('nst', 0) + 1
                        while not dry and sched and sched[0] <= st['nst']:
                            sched.pop(0)
                            pace_conv('cv%d' % a)
            for s in range(VW // 256):
                col0 = c['ov'] + 256 * s
                wt, wb = wload(('V', s), [(0, 'in', 0, DC, col0, 256)])
                wv = v3(0, DC, 256)(wt)
                for tb in range(TA // 128):
                    bank, bB = ps()
                    for kc in range(DC):
                        mm(bank[:, 0:256], xA[:, kc, tb * 128:(tb + 1) * 128], wv[:, kc, :],
                           kc == 0, kc == DC - 1, [wb, xAB], [bB])
                    n = st['vst'] % NST
                    st['vst'] += 1
                    evac(vst[n][:, :], bank[:, 0:256], [bB], [vstB[n]])
                    off = 128 + a * TA + tb * 128
                    P.dma('pool', VS[off:off + 128, 256 * s:256 * s + 256], vst[n][:, :], 'st_v%d' % n,
                          reads=[vstB[n]])
                    st['nst'] = st.get('nst', 0) + 1
                    while not dry and sched and sched[0] <= st['nst']:
                        sched.pop(0)
                        pace_conv('cv%d' % a)
        if not dry:
            pace_conv('cv%d' % (NA - 1), len(conv_pending))
        P.barrier(exclude=('cvDN',))

    iq = 0
    iyc = QC
    iat = QC + CC
    img = 2 * QC + CC
    SCALE = 1.0 / math.sqrt(128.0)

    ln_sqs = {}

    def ln_begin():
        reserved.update((6, 7))

    ln_pool = {'on': False}

    def ln_sq(o):
        sq, sqB = scr()
        act(sq[:, :], tt_[:, o, :], AF.Square, [tB[o]], [sqB])
        if ln_pool['on']:
            if o == 0:
                P.op('pool', lambda e: e.tensor_copy(ln_mean[:, :], tt_[:, 0, :]), [tB[0]], [ln_meanB])
                P.op('pool', lambda e, q=sq: e.tensor_copy(ln_rstd[:, :], q[:, :]), [sqB], [ln_rstdB])
            else:
                P.op('pool', lambda e, o=o: e.tensor_tensor(ln_mean[:, :], ln_mean[:, :], tt_[:, o, :], ALU.add),
                     [tB[o], ln_meanB], [ln_meanB])
                P.op('pool', lambda e, q=sq: e.tensor_tensor(ln_rstd[:, :], ln_rstd[:, :], q[:, :], ALU.add),
                     [sqB, ln_rstdB], [ln_rstdB])
            return
        ln_sqs[o] = (sq, sqB)

    def ln_mm(o):
        if ln_pool['on']:
            if o == DC - 1:
                mm(psum[6][:, 0:TW], ones_f[:, :], ln_mean[:, :], True, True, [constB, ln_meanB], [psB[6]])
                mm(psum[7][:, 0:TW], ones_f[:, :], ln_rstd[:, :], True, True, [constB, ln_rstdB], [psB[7]])
            return
        sq, sqB = ln_sqs.pop(o)
        mm(psum[6][:, 0:TW], ones_f[:, :], tt_[:, o, :], o == 0, o == DC - 1, [constB, tB[o]], [psB[6]])
        mm(psum[7][:, 0:TW], ones_f[:, :], sq[:, :], o == 0, o == DC - 1, [constB, sqB], [psB[7]])

    def ln_finish(m, gcol, bcol, write_bf16):
        bsum, bsumB, bsq, bsqB = psum[6], psB[6], psum[7], psB[7]
        mean, meanB = ln_mean, ln_meanB
        dts(mean[:, :], bsum[:, 0:TW], 1.0 / D, None, ALU.mult, None, [bsumB], [meanB])
        msq, msqB = scr()
        dtt(msq[:, :], mean[:, :], mean[:, :], ALU.mult, [meanB], [msqB])
        var, varB = scr()
        dstt(var[:, :], bsq[:, 0:TW], 1.0 / D, msq[:, :], ALU.mult, ALU.subtract, [bsqB, msqB], [varB])
        reserved.clear()
        rstd, rstdB = ln_rstd, ln_rstdB
        act(rstd[:, :], var[:, :], AF.Ln, [varB, constB], [rstdB], bias=eps_t[:, 0:1], scale=1.0)
        act(rstd[:, :], rstd[:, :], AF.Exp, [rstdB], [rstdB], scale=-0.5)
        fl = pcol(c['c_fl'] + m)
        tmps = {}
        for o in range(DC + 3):
            if o < DC:
                tmp, tmpB = scr()
                tmps[o] = (tmp, tmpB)
                dtt(tmp[:, :], tt_[:, o, :], mean[:, :], ALU.subtract, [tB[o], meanB], [tmpB])
            if 0 <= o - 1 < DC:
                tmp, tmpB = tmps[o - 1]
                dtt(tmp[:, :], tmp[:, :], rstd[:, :], ALU.mult, [tmpB, rstdB], [tmpB])
            if 0 <= o - 2 < DC:
                q_ = o - 2
                tmp, tmpB = tmps.pop(q_)
                act(tt_[:, q_, :], tmp[:, :], AF.Identity, [tmpB, parB], [tB[q_]],
                    bias=pcol(bcol + q_), scale=pcol(gcol + q_))
                if write_bf16:
                    act(hb[:, q_, :], tmp[:, :], AF.Identity, [tmpB, parB], [hbB[q_]],
                        bias=pcol(bcol + q_), scale=pcol(gcol + q_))
            if write_bf16 and 0 <= o - 3 < DC:
                q_ = o - 3
                dts(hb[:, q_, 0:HALO], hb[:, q_, 0:HALO], fl, None, ALU.mult, None, [hbB[q_], parB], [hbB[q_]])

    LAG = 2

    if 'B' in phases:
        for m in range(NT):
            ln_pool['on'] = m >= 1
            xt_v = xT_tiles[m].rearrange("(c p) t -> p c t", p=128)
            if m == 0:
                P.dma('pool', hb[:, :, :], xt_v, 'ld_hb', writes=hbB)
                P.dma('pool', maskt[:, :, :], masks_d[m].rearrange("p (b t) -> p b t", t=TW), 'ld_mask',
                      writes=[maskB])
            P.dma('pool', tt_[:, :, :], xt_v, 'ld_t', writes=tB)

            for s in range(QC // 2):
                col0 = c['oq'] + 256 * s
                wt, wb = wload(('q', s), [(0, 'in', 0, DC, col0, 256)])
                wv = v3(0, DC, 256)(wt)
                for j in range(2):
                    ch = 2 * s + j
                    bank, bB = ps()
                    for kc in range(DC):
                        mm(bank[:, 0:TW], wv[:, kc, j * 128:(j + 1) * 128], hb[:, kc, :], kc == 0, kc == DC - 1,
                           [wb, hbB[kc]], [bB])
                    evac(Rv[:, iq + ch, :], bank[:, 0:TW], [bB], [RB[iq + ch]])

            for s in range(CC // 2):
                col0 = c['ogb'] + 256 * s
                wt, wb = wload(('gb', s), [(0, 'in', 0, DC, col0, 256)])
                wv = v3(0, DC, 256)(wt)
                gbs = []
                for j in range(2):
                    bank, bB = ps()
                    for kc in range(DC):
                        mm(bank[:, 0:TW], wv[:, kc, j * 128:(j + 1) * 128], hb[:, kc, :], kc == 0, kc == DC - 1,
                           [wb, hbB[kc]], [bB])
                    g, gB = scr()
                    act(g[:, :], bank[:, 0:TW], AF.Copy, [bB], [gB])
                    gbs.append((g, gB))
                for j in range(2):
                    i = 2 * s + j
                    cu = c['ou'] + 128 * i
                    cg = c['ogc'] + 128 * i
                    wt, wb = wload(('ug', i), [(0, 'in', 0, DC, cu, 128), (DC * 128, 'in', 0, DC, cg, 128)])
                    wu = v3(0, DC, 128)(wt)
                    wg = v3(DC * 128, DC, 128)(wt)
                    bu, buB = ps()
                    bg, bgB = ps()
                    for kc in range(DC):
                        mm(bu[:, 0:TW], wu[:, kc, :], hb[:, kc, :], kc == 0, kc == DC - 1, [wb, hbB[kc]], [buB])
                    for kc in range(DC):
                        mm(bg[:, 0:TW], wg[:, kc, :], hb[:, kc, :], kc == 0, kc == DC - 1, [wb, hbB[kc]], [bgB])
                    us, usB = scr()
                    act(us[:, :], bu[:, 0:TW], AF.Copy, [buB], [usB])
                    gcu, gcuB = scr()
                    dtt(gcu[:, :], bg[:, 0:TW], us[:, :], ALU.mult, [bgB, usB], [gcuB])
                    cv, cvB = scr()
                    cw = c['c_cm'] + 3 * i
                    dts(cv[:, :], gcu[:, :], pcol(cw + 2), None, ALU.mult, None, [gcuB, parB], [cvB])
                    dstt(cv[:, 1:TW], gcu[:, 0:TW - 1], pcol(cw + 1), cv[:, 1:TW], ALU.mult, ALU.add,
                         [gcuB, cvB, parB], [cvB])
                    dstt(cv[:, 2:TW], gcu[:, 0:TW - 2], pcol(cw + 0), cv[:, 2:TW], ALU.mult, ALU.add,
                         [gcuB, cvB, parB], [cvB])
                    g, gB = gbs[j]
                    dtt(Rv[:, iyc + i, :], cv[:, :], g[:, :], ALU.mult, [cvB, gB], [RB[iyc + i]])

            groups = []
            for g in range(m):
                if g == 0:
                    groups.append((1, 4, False, 0))
                    groups.append((5, 3, False, 0))
                else:
                    groups.append((8 * g, 4, False, 0))
                    groups.append((8 * g + 4, 4, False, 0))
            groups.append((8 * m, 5, True, 0))
            groups.append((8 * m + 5, 4, True, 5))
            ACC = [(psum[k], psB[k]) for k in range(3)]
            SBK = [3, 4, 5, 6, 7]
            LOOK = 4
            seq = []
            for h in range(H):
                for cc in range(2):
                    for gi, (b0, nb, win, wb0) in enumerate(groups):
                        for bl in range(nb):
                            seq.append((h, cc, gi, bl))
            nper = sum(nb for (_, nb, _, _) in groups)
            loaded = {}
            info = {}
            deferred = []

            def emitS(n):
                h, cc, gi, bl = seq[n]
                b0, nb, win, wb0 = groups[gi]
                lk = (h, cc, gi)
                if lk not in loaded:
                    si = st['kv'] % NKV
                    st['kv'] += 1
                    P.dma('sp', kt_sl[si][:, 0:nb * 128], KT[2 * h + cc, :, b0 * 128:(b0 + nb) * 128],
                          'ld_kv%d' % si, writes=[kvB[si]])
                    P.dma('sp', v_sl[si][:, 0:nb, :],
                          VS[b0 * 128:(b0 + nb) * 128, 256 * h:256 * h + 256].rearrange("(b p) v -> p b v", p=128),
                          'ld_kv%d' % si, writes=[kvB[si]])
                    loaded[lk] = si
                si = loaded[lk]
                sb = SBK[st['sb'] % 5]
                st['sb'] += 1
                pi = st['pt'] % NPT
                st['pt'] += 1
                qch = iq + 2 * h + cc
                mm(psum[sb][:, 0:TW], kt_sl[si][:, bl * 128:(bl + 1) * 128], Rv[:, qch, :],
                   True, True, [kvB[si], RB[qch]], [psB[sb]])
                act(pt[pi], psum[sb][:, 0:TW], AF.Exp, [psB[sb]], [ptB[pi]], scale=SCALE)
                if win:
                    dtt(pt[pi], pt[pi], maskt[:, wb0 + bl, :], ALU.mult, [ptB[pi], maskB], [ptB[pi]])
                info[n] = (si, pi)

            def emitPV(n):
                h, cc, gi, bl = seq[n]
                si, pi = info.pop(n)
                idx = n % nper
                fs = idx == 0
                ls = idx == nper - 1
                for k in range(2):
                    mm(ACC[k][0][:, 0:TW], v_sl[si][:, bl, 128 * k:128 * k + 128], pt[pi],
                       fs, ls, [kvB[si], ptB[pi]], [ACC[k][1]])
                mm(ACC[2][0][:, 0:TW], ones_b[:, :], pt[pi], fs, ls, [constB, ptB[pi]], [ACC[2][1]])

            def epi_evac():
                t0, t0B = scr()
                t1, t1B = scr()
                rr, rrB = scr()
                P.op('dve', lambda e, o=t0, i=ACC[0][0]: e.tensor_copy(o[:, :], i[:, 0:TW]), [ACC[0][1]], [t0B])
                act(t1[:, :], ACC[1][0][:, 0:TW], AF.Copy, [ACC[1][1]], [t1B])
                P.op('dve', lambda e, o=rr, i=ACC[2][0]: e.tensor_copy(o[:, :], i[:, 0:TW]), [ACC[2][1]], [rrB])
                P.op('dve', lambda e, o=rr: e.reciprocal(o[:, :], o[:, :]), [rrB], [rrB])
                return [(t0, t0B), (t1, t1B)], rr, rrB

            def epi0(h):
                ts, rr, rrB = epi_evac()
                for k in range(2):
                    dtt(o1buf[k][:, :], ts[k][0][:, :], rr[:, :], ALU.mult, [ts[k][1], rrB], [o1B[k]])

            def epi1(h):
                ts, rr, rrB = epi_evac()
                os_ = []
                sqs = []
                for k in range(2):
                    o_, oB = ts[k]
                    dtt(o_[:, :], o_[:, :], rr[:, :], ALU.mult, [oB, rrB], [oB])
                    dstt(o_[:, :], o_[:, :], neglam, o1buf[k][:, :], ALU.mult, ALU.add, [oB, o1B[k], lamB], [oB])
                    sq, sqB = scr()
                    act(sq[:, :], o_[:, :], AF.Square, [oB], [sqB])
                    os_.append((o_, oB))
                    sqs.append((sq, sqB))

                def partB(h=h, os_=os_, sqs=sqs):
                    mb = SBK[st['sb'] % 5]
                    st['sb'] += 1
                    for k in range(2):
                        mm(psum[mb][:, 0:TW], ones_f[:, :], sqs[k][0][:, :], k == 0, k == 1, [constB, sqs[k][1]],
                           [psB[mb]])
                    rstd, rstdB = scr()
                    act(rstd[:, :], psum[mb][:, 0:TW], AF.Ln, [psB[mb], constB], [rstdB], bias=eps_t[:, 0:1],
                        scale=1.0 / 256.0)
                    act(rstd[:, :], rstd[:, :], AF.Exp, [rstdB], [rstdB], scale=-0.5)
                    for k in range(2):
                        ai = iat + 2 * h + k
                        dstt(Rv[:, ai, :], os_[k][0][:, :], lamt[:, 6 + k:7 + k], rstd[:, :], ALU.mult, ALU.mult,
                             [os_[k][1], rstdB, lamB], [RB[ai]])
                deferred.append([4, partB])

            for n in range(min(LOOK, len(seq))):
                emitS(n)
            for n in range(len(seq)):
                if n + LOOK < len(seq):
                    emitS(n + LOOK)
                emitPV(n)
                for d in deferred:
                    d[0] -= 1
                while deferred and deferred[0][0] <= 0:
                    deferred.pop(0)[1]()
                h, cc, gi, bl = seq[n]
                if n % nper == nper - 1:
                    if cc == 0:
                        epi0(h)
                    else:
                        epi1(h)
            while deferred:
                deferred.pop(0)[1]()

            if m + 1 < NT:
                P.dma('pool', maskt[:, :, :], masks_d[m + 1].rearrange("p (b t) -> p b t", t=TW), 'ld_mask',
                      writes=[maskB])
            for s in range(DC // 2):
                for j in range(2):
                    o = 2 * s + j
                    ca = c['oga'] + 128 * o
                    cg = c['ogcv'] + 128 * o
                    wt, wb = wload(('g', o), [(0, 'in', 0, DC, ca, 128), (DC * 128, 'in', 0, DC, cg, 128)])
                    wtA, wbA = wload(('ac', o), [(0, 'ao', 0, QC, 128 * o, 128), (QC * 128, 'co', 0, CC, 128 * o, 128)])
                    wa = v3(0, QC, 128)(wtA)
                    wc = v3(QC * 128, CC, 128)(wtA)
                    wga = v3(0, DC, 128)(wt)
                    wgc = v3(DC * 128, DC, 128)(wt)
                    bga, bgaB = ps()
                    bgc, bgcB = ps()
                    bya, byaB = ps()
                    byc, bycB = ps()
                    for kc in range(DC):
                        mm(bga[:, 0:TW], wga[:, kc, :], hb[:, kc, :], kc == 0, kc == DC - 1, [wb, hbB[kc]], [bgaB])
                    for kc in range(DC):
                        mm(bgc[:, 0:TW], wgc[:, kc, :], hb[:, kc, :], kc == 0, kc == DC - 1, [wb, hbB[kc]], [bgcB])
                    for kc in range(QC):
                        mm(bya[:, 0:TW], wa[:, kc, :], Rv[:, iat + kc, :], kc == 0,
                           kc == QC - 1, [wbA, RB[iat + kc]], [byaB])
                    for kc in range(CC):
                        mm(byc[:, 0:TW], wc[:, kc, :], Rv[:, iyc + kc, :], kc == 0,
                           kc == CC - 1, [wbA, RB[iyc + kc]], [bycB])
                    sa, saB = scr()
                    act(sa[:, :], bga[:, 0:TW], AF.Sigmoid, [bgaB], [saB])
                    sc_, scB = scr()
                    act(sc_[:, :], bgc[:, 0:TW], AF.Sigmoid, [bgcB], [scB])
                    m1, m1B = scr()
                    dtt(m1[:, :], bya[:, 0:TW], sa[:, :], ALU.mult, [byaB, saB], [m1B])
                    m2, m2B = scr()
                    dtt(m2[:, :], byc[:, 0:TW], sc_[:, :], ALU.mult, [bycB, scB], [m2B])
                    dtt(Rv[:, img + o, :], m1[:, :], m2[:, :], ALU.add, [m1B, m2B], [RB[img + o]])

            ln_begin()
            for s in range(DC // 2):
                c2 = 256 * s
                wt, wb = wload(('wo', s), [(0, 'o', 0, DC, c2, 256)])
                wv = v3(0, DC, 256)(wt)
                for j in range(2):
                    o = 2 * s + j
                    bank, bB = ps()
                    for kc in range(DC):
                        mm(bank[:, 0:TW], wv[:, kc, j * 128:(j + 1) * 128], Rv[:, img + kc, :], kc == 0,
                           kc == DC - 1, [wb, RB[img + kc]], [bB])
                    dstt(tt_[:, o, :], tt_[:, o, :], ALPHA, bank[:, 0:TW], ALU.mult, ALU.add, [tB[o], bB], [tB[o]])
                    ln_sq(o)
                    if o - LAG >= 0:
                        ln_mm(o - LAG)

            for o in range(max(0, DC - LAG), DC):
                ln_mm(o)
            ln_finish(m, c['c_ln'], c['c_ln'] + DC, True)

            for i in range(FC):
                cg = 128 * i
                cvv = DFF + 128 * i
                wt, wb = wload(('up', i), [(0, 'up', 0, DC, cg, 128), (DC * 128, 'up', 0, DC, cvv, 128)])
                wg = v3(0, DC, 128)(wt)
                wvv = v3(DC * 128, DC, 128)(wt)
                bg, bgB = ps()
                bv, bvB = ps()
                for kc in range(DC):
                    mm(bg[:, 0:TW], wg[:, kc, :], hb[:, kc, :], kc == 0, kc == DC - 1, [wb, hbB[kc]], [bgB])
                for kc in range(DC):
                    mm(bv[:, 0:TW], wvv[:, kc, :], hb[:, kc, :], kc == 0, kc == DC - 1, [wb, hbB[kc]], [bvB])
                res = []
                for (bk, bkB, ci) in ((bg, bgB, i), (bv, bvB, FC + i)):
                    cw = c['c_fc'] + 3 * ci
                    cvt, cvtB = scr()
                    act(cvt[:, :], bk[:, 0:TW], AF.Identity, [bkB, parB], [cvtB], scale=pcol(cw + 2))
                    dstt(cvt[:, 1:TW], bk[:, 0:TW - 1], pcol(cw + 1), cvt[:, 1:TW], ALU.mult, ALU.add,
                         [bkB, cvtB, parB], [cvtB])
                    dstt(cvt[:, 2:TW], bk[:, 0:TW - 2], pcol(cw + 0), cvt[:, 2:TW], ALU.mult, ALU.add,
                         [bkB, cvtB, parB], [cvtB])
                    res.append((cvt, cvtB))
                sg, sgB = scr()
                act(sg[:, :], res[0][0][:, :], AF.Silu, [res[0][1]], [sgB])
                dtt(Rv[:, i, :], sg[:, :], res[1][0][:, :], ALU.mult, [sgB, res[1][1]], [RB[i]])

            if m + 1 < NT:
                P.dma('pool', hb[:, :, :], xT_tiles[m + 1].rearrange("(c p) t -> p c t", p=128), 'ld_hb', writes=hbB)

            ln_begin()
            K1 = (FC + 1) // 2
            K2 = FC - K1
            for o in range(DC):
                c2 = 128 * o
                wt1, wb1 = wload(('dn', o, 0), [(0, 'dn', 0, K1, c2, 128)])
                wt2, wb2 = wload(('dn', o, 1), [(0, 'dn', K1, K2, c2, 128)])
                w1 = v3(0, K1, 128)(wt1)
                w2 = v3(0, K2, 128)(wt2)
                bank, bB = ps()
                for kc in range(FC):
                    if kc < K1:
                        mm(bank[:, 0:TW], w1[:, kc, :], Rv[:, kc, :], kc == 0, kc == FC - 1, [wb1, RB[kc]], [bB])
                    else:
                        mm(bank[:, 0:TW], w2[:, kc - K1, :], Rv[:, kc, :], kc == 0, kc == FC - 1, [wb2, RB[kc]], [bB])
                dstt(tt_[:, o, :], tt_[:, o, :], ALPHA, bank[:, 0:TW], ALU.mult, ALU.add, [tB[o], bB], [tB[o]])
                ln_sq(o)
                if o - LAG >= 0:
                    ln_mm(o - LAG)

            for o in range(max(0, DC - LAG), DC):
                ln_mm(o)
            ln_finish(m, c['c_ln'] + 2 * DC, c['c_ln'] + 3 * DC, False)
            tok = P.dma('pool', yT[m].rearrange("(c p) t -> p c t", p=128), tt_[:, :, HALO:TW], 'st_y', reads=tB)
        P.final_wait('pool', [tok])
    else:
        P.final_wait('pool', [('D', sk, v) for sk, v in P.dmacnt.items()])

    if dry:
        return rec
    P.emit()
    return nc, P


def build(cfg):
    plan = _build(cfg, None)
    nc, P = _build(cfg, plan)
    P.plan = plan
    return nc, P


def host_inputs(cfg, plan, x, w_in, lambda_q1, lambda_k1, lambda_q2, lambda_k2, subln_g, conv_mix_w,
                w_attn_out, w_conv_out, w_o, ln1_g, ln1_b, w_up, ffn_conv_w, w_down, ln2_g, ln2_b):
    c = derive(cfg)
    DC, H, CC, FC, NT = c['DC'], c['H'], c['CC'], c['FC'], c['NT']
    D, SEQ = c['D'], c['SEQ']
    x = np.asarray(x, dtype=np.float32)
    B = x.shape[0]
    assert B == 2 and x.shape[1] == SEQ and x.shape[2] == D
    f = lambda a: np.ascontiguousarray(np.asarray(a, dtype=np.float32))
    W = {'in': f(w_in[0]), 'ao': f(w_attn_out[0]), 'co': f(w_conv_out[0]),
         'o': f(w_o[0]), 'up': f(w_up[0]), 'dn': f(w_down[0])}
    WSZ = max(DC * 256, (c['QC'] + CC) * 256, ((FC + 1) // 2) * 128)
    wsrc = np.zeros((len(plan), 128, WSZ), np.float32)
    for k, (key, parts) in enumerate(plan.items()):
        for (doff, sname, kc0, nk, col0, ncol) in parts:
            blk = W[sname][kc0 * 128:(kc0 + nk) * 128, col0:col0 + ncol]
            wsrc[k, :, doff:doff + nk * ncol] = blk.reshape(nk, 128, ncol).transpose(1, 0, 2).reshape(128, nk * ncol)
    shared = {"wsrc": wsrc}
    par = np.zeros((128, c['NP']), np.float32)
    lamv = np.concatenate([f(lambda_q1[0]), f(lambda_k1[0]), f(lambda_q2[0]), f(lambda_k2[0])])
    par[:, 0:512] = lamv[None, :]
    par[:, c['c_sg']:c['c_sg'] + 2] = f(subln_g[0]).reshape(2, 128).T
    par[:, c['c_cm']:c['c_cm'] + CC * 3] = f(conv_mix_w[0]).reshape(3, CC, 128).transpose(2, 1, 0).reshape(128, CC * 3)
    par[:, c['c_fc']:c['c_fc'] + 2 * FC * 3] = f(ffn_conv_w[0]).reshape(3, 2 * FC, 128).transpose(2, 1, 0).reshape(128, 2 * FC * 3)
    lnp = np.stack([f(ln1_g[0]), f(ln1_b[0]), f(ln2_g[0]), f(ln2_b[0])])
    par[:, c['c_ln']:c['c_ln'] + 4 * DC] = lnp.reshape(4, DC, 128).transpose(2, 0, 1).reshape(128, 4 * DC)
    xT = [np.ascontiguousarray(x[b].T) for b in range(B)]
    in_maps = []
    kk = np.arange(128)[:, None, None]
    wb = np.arange(9)[None, :, None]
    qi = np.arange(TW)[None, None, :]
    for core in range(8):
        b, j = core // 4, core % 4
        tiles = np.zeros((NT, D, TW), np.float32)
        masks = np.zeros((NT, 128, 9, TW), np.float32)
        p_ = par.copy()
        for m in range(NT):
            s = (4 * m + j) * OWN
            lo = s - HALO
            if lo >= 0:
                tiles[m] = xT[b][:, lo:s + OWN]
                p_[:, c['c_fl'] + m] = 1.0
            else:
                tiles[m][:, HALO:] = xT[b][:, 0:OWN]
                p_[:, c['c_fl'] + m] = 0.0
            key = 128 * (8 * m + wb) + kk - 128
            pos = s - HALO + qi
            vis = (key >= 0) & (key <= pos)
            padq = (pos < 0) & (key < 0)
            masks[m] = (vis | padq).astype(np.float32)
        mp = dict(shared)
        mp["xT_full"] = xT[b]
        mp["xT_tiles"] = tiles
        mp["params"] = p_
        mp["masks"] = masks.reshape(NT, 128, 9 * TW).astype(ml_dtypes.bfloat16)
        in_maps.append(mp)
    return in_maps


def assemble(cfg, results):
    c = derive(cfg)
    NT, D, SEQ = c['NT'], c['D'], c['SEQ']
    out = np.zeros((2, SEQ, D), np.float32)
    for core in range(8):
        b, j = core // 4, core % 4
        y = np.asarray(results[core]["yT"])
        for m in range(NT):
            s = (4 * m + j) * OWN
            out[b, s:s + OWN, :] = y[m].T
    return out


def kernel(**inputs):
    cfg = REAL_CFG
    nc, P = build(cfg)
    in_maps = host_inputs(cfg, P.plan, **inputs)
    res = run_bass_kernel_spmd(nc, in_maps, core_ids=list(range(8)))
    return assemble(cfg, res.results)
```
